# Optimizing a Trainium2 kernel written in Bass

```python
import jax
import jax.numpy as jnp
from jax import lax
import numpy as np

D_MODEL = 2048
BATCH = 4
SEQ = 4096
DEPTH = 1

CHUNK = 64
SUB_CHUNK = 16
CONV_WIDTH = 4
GDN_HEAD_DIM = 128
GDN_HEADS = D_MODEL // (2 * GDN_HEAD_DIM)
GDN_WIDTH = GDN_HEADS * GDN_HEAD_DIM
HGRN_HEAD_DIM = 128
HGRN_VALUE_DIM = 128
HGRN_HEADS = D_MODEL // (2 * HGRN_VALUE_DIM)
HGRN_WIDTH = HGRN_HEADS * HGRN_HEAD_DIM
HGRN_V_WIDTH = HGRN_HEADS * HGRN_VALUE_DIM
D_MIX = GDN_WIDTH + HGRN_V_WIDTH
IN_PROJ_WIDTH = 4 * GDN_WIDTH + 2 * GDN_HEADS + 2 * HGRN_WIDTH + 2 * HGRN_V_WIDTH
D_FF = 4 * D_MODEL
NORM_EPS = 1e-6
L2_EPS = 1e-6

kernel_name = 'hybrid_gdn_hgrn2_block'


def rms_norm(x, w):
    xf = x.astype(jnp.float32)
    y = xf * lax.rsqrt(jnp.mean(xf * xf, axis=-1, keepdims=True) + NORM_EPS)
    return (y * w.astype(jnp.float32)).astype(x.dtype)


def head_rms_norm(o, w):
    return o * lax.rsqrt(jnp.mean(o * o, axis=-1, keepdims=True) + NORM_EPS) * w.astype(jnp.float32)


def l2_normalize(t):
    return t * lax.rsqrt(jnp.sum(t * t, axis=-1, keepdims=True) + L2_EPS)


def to_heads(t, n_heads):
    b, s, _ = t.shape
    return t.reshape(b, s, n_heads, -1).transpose(0, 2, 1, 3).astype(jnp.float32)


def from_heads(t):
    b, h, s, d = t.shape
    return t.transpose(0, 2, 1, 3).reshape(b, s, h * d)


def causal_depthwise_conv(t, w):
    return lax.conv_general_dilated(
        t, w[:, None, :].astype(t.dtype), window_strides=(1,),
        padding=[(w.shape[0] - 1, 0)], dimension_numbers=('NWC', 'WIO', 'NWC'),
        feature_group_count=t.shape[-1])


def gated_delta_rule_chunked(q, k, v, beta, g):
    b_, h_, t_, dk = q.shape
    dv = v.shape[-1]
    n_chunks = t_ // CHUNK
    q, k, v, beta, g = (t.reshape(b_, h_, n_chunks, CHUNK, *t.shape[3:]) for t in (q, k, v, beta, g))
    G = jnp.cumsum(g, axis=-1)
    pos = jnp.arange(CHUNK)
    incl = pos[:, None] >= pos[None, :]
    strict = pos[:, None] > pos[None, :]
    decay = jnp.exp(jnp.where(incl, G[..., :, None] - G[..., None, :], -jnp.inf))
    kk = jnp.einsum('bhncd,bhnsd->bhncs', k, k)
    unit_lower = jnp.where(strict, beta[..., :, None] * kk * decay, 0.0) + jnp.eye(CHUNK, dtype=jnp.float32)
    rhs = beta[..., None] * jnp.concatenate([v, jnp.exp(G)[..., None] * k], axis=-1)
    sol = lax.linalg.triangular_solve(unit_lower, rhs, left_side=True, lower=True, unit_diagonal=True)
    u_v, w = sol[..., :dv], sol[..., dv:]
    attn = jnp.einsum('bhncd,bhnsd->bhncs', q, k) * decay
    q_g = q * jnp.exp(G)[..., None]
    g_last = G[..., -1:]
    k_end = k * jnp.exp(g_last - G)[..., None]
    state_decay = jnp.exp(G[..., -1])

    def step(S, xs):
        u_v_c, w_c, q_c, attn_c, k_c, sd_c = xs
        u = u_v_c - jnp.einsum('bhcd,bhde->bhce', w_c, S)
        o = jnp.einsum('bhcd,bhde->bhce', q_c, S) + jnp.einsum('bhcs,bhse->bhce', attn_c, u)
        S = S * sd_c[..., None, None] + jnp.einsum('bhcd,bhce->bhde', k_c, u)
        return S, o

    xs = tuple(jnp.moveaxis(t, 2, 0) for t in (u_v, w, q_g, attn, k_end, state_decay))
    s0 = jnp.zeros((b_, h_, dk, dv), jnp.float32)
    _, o = lax.scan(step, s0, xs)
    return jnp.moveaxis(o, 0, 2).reshape(b_, h_, t_, dv)


def hgrn2_chunked(q, k, v, log_f):
    b_, h_, t_, dk = q.shape
    dv = v.shape[-1]
    n_chunks = t_ // CHUNK
    n_sub = CHUNK // SUB_CHUNK
    b_cum = jnp.cumsum(log_f.reshape(b_, h_, n_chunks, CHUNK, dk), axis=3)

    def blocks(t):
        return jnp.moveaxis(t.reshape(b_, h_, n_chunks, n_sub, SUB_CHUNK, t.shape[-1]), 2, 0)

    sub = jnp.arange(SUB_CHUNK)
    diag_mask = (sub[:, None] >= sub[None, :])[:, :, None]
    blk = jnp.arange(n_sub)
    off_mask = (blk[:, None] > blk[None, :])[:, :, None]

    def step(S, xs):
        qc, kc, vc, bc = xs
        b_end = bc[..., -1, :]
        b_start = jnp.concatenate([jnp.zeros_like(b_end[..., :1, :]), b_end[..., :-1, :]], axis=-2)
        o_inter = jnp.einsum('bhsid,bhde->bhsie', qc * jnp.exp(bc), S)
        d_diag = jnp.exp(jnp.where(diag_mask, bc[..., :, None, :] - bc[..., None, :, :], -jnp.inf))
        a_diag = jnp.einsum('bhsid,bhsijd,bhsjd->bhsij', qc, d_diag, kc)
        q_rel = qc * jnp.exp(bc - b_start[..., None, :])
        k_rel = kc * jnp.exp(b_end[..., None, :] - bc)
        d_off = jnp.exp(jnp.where(off_mask, b_start[..., :, None, :] - b_end[..., None, :, :], -jnp.inf))
        a_off = jnp.einsum('bhxid,bhxyd,bhyjd->bhxyij', q_rel, d_off, k_rel)
        o = (o_inter + jnp.einsum('bhsij,bhsje->bhsie', a_diag, vc)
             + jnp.einsum('bhxyij,bhyje->bhxie', a_off, vc))
        b_last = b_end[..., -1, :]
        k_state = kc * jnp.exp(b_last[:, :, None, None, :] - bc)
        S = S * jnp.exp(b_last)[..., None] + jnp.einsum('bhsjd,bhsje->bhde', k_state, vc)
        return S, o

    xs = (blocks(q), blocks(k), blocks(v), blocks(b_cum))
    s0 = jnp.zeros((b_, h_, dk, dv), jnp.float32)
    _, o = lax.scan(step, s0, xs)
    return jnp.moveaxis(o, 0, 2).reshape(b_, h_, t_, dv)


def gated_deltanet_group(qkv, z, a, b, conv_w, a_log, dt_bias, norm_w):
    qkv = jax.nn.silu(causal_depthwise_conv(qkv, conv_w))
    q, k, v = jnp.split(qkv, 3, axis=-1)
    q = l2_normalize(to_heads(q, GDN_HEADS)) * (GDN_HEAD_DIM ** -0.5)
    k = l2_normalize(to_heads(k, GDN_HEADS))
    v = to_heads(v, GDN_HEADS)
    beta = jax.nn.sigmoid(b.astype(jnp.float32)).transpose(0, 2, 1)
    g = (-jnp.exp(a_log.astype(jnp.float32))
         * jax.nn.softplus(a.astype(jnp.float32) + dt_bias.astype(jnp.float32))).transpose(0, 2, 1)
    o = gated_delta_rule_chunked(q, k, v, beta, g)
    o = head_rms_norm(o, norm_w) * jax.nn.silu(to_heads(z, GDN_HEADS))
    return from_heads(o)


def hgrn2_group(q, f, i, g, lower_bound, norm_w):
    lb = lower_bound.reshape(HGRN_HEADS, 1, HGRN_HEAD_DIM)
    f_logit = to_heads(f, HGRN_HEADS)
    forget = lb + (1.0 - lb) * jax.nn.sigmoid(f_logit)
    key = (1.0 - lb) * jax.nn.sigmoid(-f_logit)
    o = hgrn2_chunked(jax.nn.silu(to_heads(q, HGRN_HEADS)), key, to_heads(i, HGRN_HEADS), jnp.log(forget))
    o = head_rms_norm(o, norm_w) * jax.nn.silu(to_heads(g, HGRN_HEADS))
    return from_heads(o)


def hybrid_token_mixer(n, w_in, conv_w, a_log, dt_bias, gdn_norm_w, lower_bound, hgrn_norm_w, w_out):
    proj = n @ w_in
    o1 = 3 * GDN_WIDTH
    o2 = o1 + GDN_WIDTH
    o3 = o2 + GDN_HEADS
    o4 = o3 + GDN_HEADS
    o5 = o4 + HGRN_WIDTH
    o6 = o5 + HGRN_WIDTH
    o7 = o6 + HGRN_V_WIDTH
    qkv_a, z_a, a_a, b_a, q_b, f_b, i_b, g_b = jnp.split(proj, [o1, o2, o3, o4, o5, o6, o7], axis=-1)
    y_a = gated_deltanet_group(qkv_a, z_a, a_a, b_a, conv_w, a_log, dt_bias, gdn_norm_w)
    y_b = hgrn2_group(q_b, f_b, i_b, g_b, lower_bound, hgrn_norm_w)
    y = jnp.concatenate([y_a, y_b], axis=-1).astype(n.dtype)
    return y @ w_out


def squared_relu_mlp(n, w1, w2):
    return jnp.square(jax.nn.relu(n @ w1)) @ w2


def setup_inputs(seed: int = 0) -> dict:
    key = jax.random.key(seed)
    ks = jax.random.split(key, 16)
    f32 = jnp.float32
    x = jax.random.normal(ks[0], (BATCH, SEQ, D_MODEL), f32)
    w_in = jax.random.normal(ks[1], (DEPTH, D_MODEL, IN_PROJ_WIDTH), f32) * D_MODEL ** -0.5
    conv_w = jax.random.normal(ks[2], (DEPTH, CONV_WIDTH, 3 * GDN_WIDTH), f32) * CONV_WIDTH ** -0.5
    gdn_a_log = jnp.log(jax.random.uniform(ks[3], (DEPTH, GDN_HEADS), f32, 1.0, 16.0))
    dt = jnp.exp(jax.random.uniform(ks[4], (DEPTH, GDN_HEADS), f32, np.log(1e-3), np.log(1e-1)))
    gdn_dt_bias = dt + jnp.log(-jnp.expm1(-dt))
    gdn_norm_w = 1.0 + 0.02 * jax.random.normal(ks[5], (DEPTH, GDN_HEAD_DIM), f32)
    hgrn_lb_logits = 0.1 * jax.random.normal(ks[6], (DEPTH + 1, HGRN_WIDTH), f32)
    hgrn_norm_w = 1.0 + 0.02 * jax.random.normal(ks[7], (DEPTH, HGRN_VALUE_DIM), f32)
    w_out = jax.random.normal(ks[8], (DEPTH, D_MIX, D_MODEL), f32) * D_MIX ** -0.5
    norm_mix_w = 1.0 + 0.02 * jax.random.normal(ks[9], (DEPTH, D_MODEL), f32)
    norm_ffn_w = 1.0 + 0.02 * jax.random.normal(ks[10], (DEPTH, D_MODEL), f32)
    w_ff1 = jax.random.normal(ks[11], (DEPTH, D_MODEL, D_FF), f32) * D_MODEL ** -0.5
    w_ff2 = jax.random.normal(ks[12], (DEPTH, D_FF, D_MODEL), f32) * D_FF ** -0.5
    norm_final_w = 1.0 + 0.02 * jax.random.normal(ks[13], (D_MODEL,), f32)
    return {'x': x, 'w_in': w_in, 'conv_w': conv_w, 'gdn_a_log': gdn_a_log,
            'gdn_dt_bias': gdn_dt_bias, 'gdn_norm_w': gdn_norm_w, 'hgrn_lb_logits': hgrn_lb_logits,
            'hgrn_norm_w': hgrn_norm_w, 'w_out': w_out, 'norm_mix_w': norm_mix_w,
            'norm_ffn_w': norm_ffn_w, 'w_ff1': w_ff1, 'w_ff2': w_ff2, 'norm_final_w': norm_final_w}


def reference(x, w_in, conv_w, gdn_a_log, gdn_dt_bias, gdn_norm_w, hgrn_lb_logits, hgrn_norm_w,
              w_out, norm_mix_w, norm_ffn_w, w_ff1, w_ff2, norm_final_w):
    lower_bounds = jnp.cumsum(jax.nn.softmax(hgrn_lb_logits.astype(jnp.float32), axis=0), axis=0)
    h = x
    for layer in range(DEPTH):
        n = rms_norm(h, norm_mix_w[layer])
        h = h + hybrid_token_mixer(n, w_in[layer], conv_w[layer], gdn_a_log[layer], gdn_dt_bias[layer],
                                   gdn_norm_w[layer], lower_bounds[layer], hgrn_norm_w[layer], w_out[layer])
        n = rms_norm(h, norm_ffn_w[layer])
        h = h + squared_relu_mlp(n, w_ff1[layer], w_ff2[layer])
    return rms_norm(h, norm_final_w)
```

```python
import os
from contextlib import ExitStack

import numpy as np
import ml_dtypes

import concourse.bass as bass
import concourse.mybir as mybir
from concourse.bass_utils import run_bass_kernel_spmd

F32 = mybir.dt.float32
BF16 = mybir.dt.bfloat16
AF = mybir.ActivationFunctionType
ALU = mybir.AluOpType

D = 2048
SEQ = 4096
NBLK = SEQ // 128
DFF = 8192
NORM_EPS = 1e-6
L2_EPS = 1e-6
ENGS = ("pe", "act", "dve", "pool", "sp")


class Buf:
    __slots__ = ("name", "writer", "readers", "dsem", "dcnt", "dcnt_prev", "excl")

    def __init__(self, name, excl=False):
        self.name = name
        self.excl = excl
        self.writer = None
        self.readers = []
        self.dsem = None
        self.dcnt = 0
        self.dcnt_prev = 0


class Op:
    __slots__ = ("idx", "eng", "emit", "deps", "has_dep", "inc", "sem", "is_dma", "cost", "alldeps")

    def __init__(self, idx, eng, emit, deps, is_dma):
        self.idx = idx
        self.eng = eng
        self.emit = emit
        self.deps = deps
        self.has_dep = False
        self.inc = 0
        self.sem = None
        self.is_dma = is_dma
        self.cost = 0.0
        self.alldeps = deps


class _FakeIns:
    def then_inc(self, *a, **k):
        return self


class _FakeEng:
    def __init__(self):
        self.cost = 0.0
        self.eng = "dve"

    def __getattr__(self, name):
        def call(*args, **kw):
            out = kw.get("out", args[0] if args else None)
            try:
                free = out.free_size()
            except Exception:
                free = 128
            if name == "matmul":
                rhs = kw.get("rhs")
                n = max(rhs.free_size(), 64)
                t = n / 2300.0 + 0.05
                if rhs.dtype == F32:
                    t *= float(os.environ.get("MK_F32X", 2))
            elif name == "transpose":
                t = 0.09 if kw.get("in_").dtype != F32 else 0.16
            elif name == "dma_start":
                try:
                    nb = out.nbytes()
                except Exception:
                    nb = 1 << 20
                t = 2.0 + nb / 200e3
            elif name == "activation":
                t = 0.25 + free / 1300.0
            elif self.eng == "pool":
                t = 0.3 + free / 450.0
            else:
                t = 0.12 + free / 900.0
            self.cost += t
            return _FakeIns()
        return call


class Sched:
    def __init__(self, nc, stack, same_engine_sync=True):
        self.nc = nc
        self.stack = stack
        self.ops = []
        self.base = 0
        self.same_engine_sync = same_engine_sync
        self.reorder = os.environ.get("MK_REORDER", "1") == "1"
        self.eng_sem = {e: stack.enter_context(nc.semaphore("sem_" + e)) for e in ENGS}
        self.eng_cnt = {e: 0 for e in ENGS}
        self.streams = []
        self.known = {e: {} for e in ENGS}
        self.n_wait = 0
        self.n_ops = {e: 0 for e in ENGS}

    def add(self, eng, emit, reads=(), writes=(), stream=None):
        idx = len(self.ops)
        deps = set()
        for b in reads:
            if b.writer is not None:
                deps.add(b.writer)
        for b in writes:
            if b.writer is not None:
                deps.add(b.writer)
            deps.update(b.readers)
        op = Op(idx, eng, emit, deps, stream is not None)
        op.alldeps = frozenset(deps)
        self.ops.append(op)
        for b in reads:
            b.readers.append(idx)
        for b in writes:
            b.writer = idx
            b.readers = []
        if stream is not None:
            if stream.dsem is None:
                stream.dsem = self.stack.enter_context(self.nc.semaphore("ds%d_%s" % (len(self.streams), stream.name)))
                self.streams.append(stream)
            stream.dcnt += 16
            op.sem = stream.dsem
            op.inc = stream.dcnt
        return op

    def list_schedule(self, seg, base):
        import heapq
        ops = self.ops
        fake = _FakeEng()
        for op in seg:
            if op.emit is not None:
                fake.cost = 0.0
                fake.eng = op.eng
                op.emit(fake)
                op.cost = fake.cost
            else:
                op.cost = 0.0
        idx0 = base
        n = len(seg)
        succ = [[] for _ in range(n)]
        indeg = [0] * n
        for op in seg:
            for d in op.alldeps:
                if d >= base:
                    succ[d - idx0].append(op.idx - idx0)
                    indeg[op.idx - idx0] += 1
        cp = [0.0] * n
        for i in range(n - 1, -1, -1):
            m = 0.0
            for j in succ[i]:
                if cp[j] > m:
                    m = cp[j]
            cp[i] = seg[i].cost + m
        LAT = 0.15
        WIN = float(os.environ.get("MK_WIN", 0.3))
        eng_free = {e: 0.0 for e in ENGS}
        ready_t = [0.0] * n
        finish = [0.0] * n
        ready = {e: [] for e in ENGS}
        for i in range(n):
            if indeg[i] == 0:
                ready[seg[i].eng].append(i)
        order = []
        done = 0
        while done < n:
            best = None
            for e in ENGS:
                lst = ready[e]
                if not lst:
                    continue
                t_e = eng_free[e]
                m = min(max(t_e, ready_t[i]) for i in lst)
                cand = None
                for i in lst:
                    st = max(t_e, ready_t[i])
                    if st <= m + WIN:
                        key = (-cp[i], i)
                        if cand is None or key < cand[0]:
                            cand = (key, i, st)
                if best is None or cand[2] < best[2] - 1e-9 or (abs(cand[2] - best[2]) <= 1e-9 and cand[0] < best[0]):
                    best = cand
            _, i, st = best
            op = seg[i]
            ready[op.eng].remove(i)
            fin = st + op.cost
            if op.is_dma:
                eng_free[op.eng] = st + 0.1
            else:
                eng_free[op.eng] = fin
            finish[i] = fin
            order.append(op)
            done += 1
            for j in succ[i]:
                indeg[j] -= 1
                if fin + LAT > ready_t[j]:
                    ready_t[j] = fin + LAT
                if indeg[j] == 0:
                    ready[seg[j].eng].append(j)
        self.est_time = getattr(self, "est_time", 0.0) + (max(finish) if n else 0.0)
        if os.environ.get("MK_CHAIN") and n:
            i = max(range(n), key=lambda k: cp[k])
            chain = []
            while True:
                chain.append(i)
                if not succ[i]:
                    break
                i = max(succ[i], key=lambda k: cp[k])
            import collections
            agg = collections.OrderedDict()
            for i in chain:
                op = seg[i]
                ln = op.emit.__code__.co_firstlineno if op.emit is not None else -1
                k = (op.eng, ln)
                a = agg.setdefault(k, [0, 0.0])
                a[0] += 1
                a[1] += op.cost
            print("[chain] len=%d" % len(chain))
            for k, a in sorted(agg.items(), key=lambda kv: -kv[1][1])[:25]:
                print("   ", k, a[0], "%.1f" % a[1])
        if os.environ.get("MK_VERBOSE"):
            busy = {e: 0.0 for e in ENGS}
            for op in seg:
                if not op.is_dma:
                    busy[op.eng] += op.cost
            print("[sched] segment ops=%d est=%.0fus cp=%.0fus busy=%s" % (n, max(finish) if n else 0, max(cp) if n else 0,
                                                                     {e: int(v) for e, v in busy.items()}))
        return order

    def flush(self):
        nc = self.nc
        ops = self.ops
        base = self.base
        seg = ops[base:]
        barrier = []
        if base > 0:
            for e in ENGS:
                if self.eng_cnt[e] > 0:
                    barrier.append((self.eng_sem[e], self.eng_cnt[e]))
            for s in self.streams:
                barrier.append((s.dsem, s.dcnt_prev))
        for op in seg:
            keep = set()
            for d in op.deps:
                if d < base:
                    continue
                dop = ops[d]
                if dop.eng == op.eng and not dop.is_dma and not op.is_dma:
                    if op.eng == "pe" or not self.same_engine_sync:
                        continue
                keep.add(d)
            op.deps = keep
            for d in keep:
                ops[d].has_dep = True
        if self.reorder:
            seg = self.list_schedule(seg, base)
        by_eng = {e: [] for e in ENGS}
        for op in seg:
            by_eng[op.eng].append(op)
        for e in ENGS:
            for op in reversed(by_eng[e]):
                if not op.is_dma and op.emit is not None:
                    op.has_dep = True
                    break
        for op in seg:
            if op.is_dma or not op.has_dep or op.emit is None:
                continue
            self.eng_cnt[op.eng] += 1
            op.sem = self.eng_sem[op.eng]
            op.inc = self.eng_cnt[op.eng]

        def body_for(name):
            def body(e):
                known = self.known[name]
                for sem, val in barrier:
                    if val > 0 and known.get(id(sem), 0) < val:
                        e.wait_ge(sem, val)
                        known[id(sem)] = val
                for op in by_eng[name]:
                    need = {}
                    for d in op.deps:
                        dop = ops[d]
                        k = id(dop.sem)
                        if k not in need or need[k][1] < dop.inc:
                            need[k] = (dop.sem, dop.inc)
                    for k, (sem, val) in need.items():
                        if known.get(k, 0) < val:
                            e.wait_ge(sem, val)
                            known[k] = val
                            self.n_wait += 1
                    if op.emit is None:
                        continue
                    ins = op.emit(e)
                    self.n_ops[name] += 1
                    if op.is_dma:
                        ins.then_inc(op.sem, 16)
                    elif op.has_dep:
                        ins.then_inc(op.sem, 1)
            return body

        with nc.Block() as block:
            block.tensor(body_for("pe"))
            block.scalar(body_for("act"))
            block.vector(body_for("dve"))
            block.gpsimd(body_for("pool"))
            block.sync(body_for("sp"))
        self.base = len(ops)
        for s in self.streams:
            s.dcnt_prev = s.dcnt


class T:
    __slots__ = ("t", "b")

    def __init__(self, t, b):
        self.t = t
        self.b = b


class RR:
    def __init__(self, items):
        self.items = items
        self.i = 0

    def next(self):
        it = self.items[self.i % len(self.items)]
        self.i += 1
        return it


C_IDENT, C_TRI, C_REVX, C_NEGUS, C_NEGLS, C_BD, C_ONES, C_CSEL, C_NHALF = range(9)
NCONST = 9


def make_consts():
    c = np.zeros((128, NCONST, 128), np.float32)
    p = np.arange(128)[:, None]
    f = np.arange(128)[None, :]
    same = (p // 64) == (f // 64)
    c[:, C_IDENT] = (p == f)
    c[:, C_TRI] = same & (p <= f)
    c[:, C_REVX] = same & (p > f)
    c[:, C_NEGUS] = -1.0 * (same & (p < f))
    c[:, C_NEGLS] = -1.0 * (same & (f < p))
    c[:, C_BD] = same
    c[:, C_ONES] = 1.0
    c[:, C_CSEL, 0] = (np.arange(128) < 64)
    c[:, C_CSEL, 1] = (np.arange(128) >= 64)
    c[:, C_NHALF] = -0.5
    return c


class Builder:
    def __init__(self, n_hh, n_th, debug=False):
        self.n_hh = n_hh
        self.n_th = n_th
        self.debug = debug
        self.nc = bass.Bass("TRN2", target_bir_lowering=False)
        self.uid = 0
        self.nblk = int(os.environ.get("MK_NBLK", NBLK))
        self.npre = int(os.environ.get("MK_NPRE", 0))
        self.ntt = int(os.environ.get("MK_NTT", 4))
        self.phases = os.environ.get("MK_PH", "gh2")

    def dram_in(self, name, shape, dt=F32):
        return self.nc.dram_tensor(name, list(shape), dt, kind="ExternalInput").ap()

    def sb(self, st, name, shape, dt, nbuf=None):
        self.uid += 1
        t = st.enter_context(self.nc.sbuf_tensor("%s_%d" % (name, self.uid), list(shape), dt))
        return T(t, Buf(name))

    def sbpool(self, st, name, shape, dt, n):
        return RR([self.sb(st, "%s%d" % (name, i), shape, dt) for i in range(n)])

    def ps(self, st, name, shape, dt):
        self.uid += 1
        t = st.enter_context(self.nc.psum_tensor("%s_%d" % (name, self.uid), list(shape), dt))
        return t

    def psb(self, st, name, shape, dt):
        return T(self.ps(st, name, shape, dt), Buf(name, excl=True))

    def op(self, eng, fn, reads=(), writes=(), stream=None):
        r = [x.b if isinstance(x, T) else x for x in reads]
        w = [x.b if isinstance(x, T) else x for x in writes]
        w = w + [b for b in r if b.excl and b not in w]
        r = [b for b in r if not b.excl]
        s = stream.b if isinstance(stream, T) else stream
        return self.S.add(eng, fn, reads=r, writes=w, stream=s)

    def build(self):
        nc = self.nc
        n_hh, n_th = self.n_hh, self.n_th
        self.x = self.dram_in("x", [SEQ, D])
        self.xres = self.dram_in("xres", [n_th * 2048, D])
        self.wg = [self.dram_in("wg%d" % i, [D, 2056]) for i in range(n_hh)]
        self.wh = [self.dram_in("wh%d" % i, [D, 2048]) for i in range(n_hh)]
        self.convw = [self.dram_in("convw%d" % i, [128, 12, 4]) for i in range(n_hh)]
        self.gpar = [self.dram_in("gpar%d" % i, [128, 8]) for i in range(n_hh)]
        self.lbl = [self.dram_in("lbl%d" % i, [128, 2, 512]) for i in range(n_hh)]
        self.gnw = self.dram_in("gnw", [128, 512])
        self.hnw = self.dram_in("hnw", [128, 512])
        self.wout = self.dram_in("wout", [D, D])
        self.w1 = self.dram_in("w1", [D, DFF])
        self.w2 = self.dram_in("w2", [DFF, D])
        self.nmix = self.dram_in("nmix", [128, D])
        self.nffn = self.dram_in("nffn", [128, D])
        self.nfin = self.dram_in("nfin", [128, D])
        self.consts_d = self.dram_in("consts", [128, NCONST, 128])
        self.out = nc.dram_tensor("out", [n_th * 2048, D], F32, kind="ExternalOutput").ap()
        ykind = {"kind": "ExternalOutput"} if self.debug else {}
        n_hd = 8 * n_hh
        self.n_hd = n_hd
        self.yscr = nc.dram_tensor("yscr", [NBLK, 128, 16, 128], BF16, **ykind).ap()
        self.b_yscr = [[Buf("yscr_%d_%d" % (n, q)) for q in range(4)] for n in range(NBLK)]
        self.wo_s = nc.dram_tensor("wo_s", [8, 128, 16, 256], BF16).ap()
        self.w1_s = nc.dram_tensor("w1_s", [32, 128, 16, 256], BF16).ap()
        self.w2_s = nc.dram_tensor("w2_s", [4, 16, 128, 4, 512], BF16).ap()
        self.b_pre = Buf("precast")

        self.pre_pending = self.precast_list()
        with ExitStack() as st:
            self.S = Sched(nc, st)
            self.cst = self.sb(st, "cst", [128, NCONST, 128], F32)
            self.identb = self.sb(st, "identb", [128, 128], BF16)
            self.onesb = self.sb(st, "onesb", [128, 128], BF16)
            self.op("sp", lambda e: e.dma_start(out=self.cst.t[:], in_=self.consts_d), writes=[self.cst], stream=self.cst)
            self.op("dve", lambda e: e.tensor_copy(out=self.identb.t[:], in_=self.cst.t[:, C_IDENT, :]), reads=[self.cst], writes=[self.identb])
            self.op("dve", lambda e: e.tensor_copy(out=self.onesb.t[:], in_=self.cst.t[:, C_ONES, :]), reads=[self.cst], writes=[self.onesb])
            self.epsn = self.sb(st, "epsn", [128, 2], F32)
            self.op("pool", lambda e: e.memset(self.epsn.t[:, 0:1], NORM_EPS), writes=[self.epsn])
            self.op("pool", lambda e: e.memset(self.epsn.t[:, 1:2], L2_EPS), writes=[self.epsn])
            first = True
            with ExitStack() as p1st:
                self.Wt = self.sb(p1st, "W", [128, 16, 2056], BF16)
                self.Wb = [Buf("Wq%d" % i) for i in range(4)]
                plist = []
                for hh in range(n_hh):
                    if "g" in self.phases:
                        plist.append(("g", hh))
                    if "h" in self.phases:
                        plist.append(("h", hh))
                self.w_loaded = False
                for i, (kind, hh) in enumerate(plist):
                    nxt = plist[i + 1] if i + 1 < len(plist) else None
                    self.next_pass = nxt
                    if kind == "g":
                        self.phase1_gdn(hh, precast=first and "2" in self.phases)
                        first = False
                    else:
                        self.phase1_hgrn(hh)
                    self.S.flush()
            for th in range(n_th):
                if "2" in self.phases:
                    if first:
                        self.precast_weights()
                        first = False
                    self.phase2(th)
                    self.S.flush()
            if "2" not in self.phases:
                pass
        return nc

    def cpl(self, i):
        return self.cst.t[:, i, :]

    def precast_list(self):
        lst = []
        for g in range(8):
            lst.append((self.wo_s[g], self.wout[:, g * 256:(g + 1) * 256].rearrange("(kc p) c -> p kc c", p=128)))
        for g in range(32):
            lst.append((self.w1_s[g], self.w1[:, g * 256:(g + 1) * 256].rearrange("(kc p) c -> p kc c", p=128)))
        for i in range(16):
            for cg in range(4):
                lst.append((self.w2_s[cg, i], self.w2[i * 512:(i + 1) * 512, cg * 512:(cg + 1) * 512].rearrange("(f p) c -> p f c", p=128)))
        return lst

    def precast_some(self, k, after=None):
        for _ in range(k):
            if not self.pre_pending:
                return
            dst, src = self.pre_pending.pop(0)
            self.op("pool", lambda e, dst=dst, src=src: e.dma_start(out=dst, in_=src), reads=[after] if after is not None else [],
                    stream=self.b_pre)

    def precast_weights(self):
        pre = self.b_pre
        return self.precast_some(1000)

    def precast_weights_old(self):
        pre = self.b_pre
        for g in range(8):
            src = self.wout[:, g * 256:(g + 1) * 256].rearrange("(kc p) c -> p kc c", p=128)
            self.op("pool", lambda e, g=g, src=src: e.dma_start(out=self.wo_s[g], in_=src), stream=pre)
        for g in range(32):
            src = self.w1[:, g * 256:(g + 1) * 256].rearrange("(kc p) c -> p kc c", p=128)
            self.op("pool", lambda e, g=g, src=src: e.dma_start(out=self.w1_s[g], in_=src), stream=pre)
        for i in range(16):
            for cg in range(4):
                src = self.w2[i * 512:(i + 1) * 512, cg * 512:(cg + 1) * 512].rearrange("(f p) c -> p f c", p=128)
                self.op("pool", lambda e, i=i, cg=cg, src=src: e.dma_start(out=self.w2_s[cg, i], in_=src), stream=pre)

    def p1_common_alloc(self, st, ncols):
        c = {}
        c["W"] = self.Wt
        c["Wb"] = self.Wb
        c["wbc"] = self.sb(st, "wbc", [128, D], F32)
        c["xt"] = self.sbpool(st, "xt", [128, D], F32, 2)
        c["nb"] = self.sbpool(st, "nb", [128, D], BF16, 1)
        c["nT"] = self.sbpool(st, "nT", [128, 16, 128], BF16, 2)
        c["ss"] = self.sbpool(st, "ss", [128, 4], F32, 2)
        c["nwbc"] = self.sb(st, "nwbc", [128, 512], F32)
        if ncols == 2056:
            nA, nTB, nGF, nGN, nGS = [int(v) for v in os.environ.get("MK_BANKS", "1,1,2,2,1").split(",")]
        else:
            nA, nTB, nGF, nGN, nGS = [int(v) for v in os.environ.get("MK_BANKS_H", "2,2,2,0,1").split(",")]
        c["A"] = RR([self.psb(st, "A%d" % i, [128, 512], F32) for i in range(nA)])
        c["TB"] = RR([self.psb(st, "TB%d" % i, [128, 1024], BF16) for i in range(nTB)])
        c["GF"] = RR([self.psb(st, "GF%d" % i, [128, 512], F32) for i in range(nGF)])
        c["GN"] = RR([self.psb(st, "GN%d" % i, [128, 512], F32) for i in range(nGN)]) if nGN else c["GF"]
        c["GS"] = RR([self.psb(st, "GS%d" % i, [128, 512], F32) for i in range(nGS)]) if nGS else c["GF"]
        c["G"] = c["GF"]
        c["OPS"] = self.psb(st, "OPS", [128, 512], F32)
        c["S32"] = self.sb(st, "S32", [128, 4, 128], F32)
        c["Sbf"] = self.sb(st, "Sbf", [128, 4, 128], BF16)
        c["Stmp"] = self.sb(st, "Stmp", [128, 4, 128], F32)
        c["osq"] = self.sb(st, "osq", [128, 4, 128], BF16)
        c["ot"] = self.sb(st, "ot", [128, 4, 128], F32)
        c["ycol"] = self.sbpool(st, "ycol", [128, 12], F32, 2)
        c["ytm"] = self.sbpool(st, "ytm", [128, 4, 128], BF16, 2)
        c["yT"] = self.sbpool(st, "yT", [128, 4, 128], BF16, 2)
        return c

    def p1_stream_w(self, wdram, ncols):
        W = self.Wt
        wv = wdram.rearrange("(kc p) c -> p kc c", p=128)
        for i in range(4):
            self.op("pool", lambda e, i=i: e.dma_start(out=W.t[:, 4 * i:4 * i + 4, 0:ncols], in_=wv[:, 4 * i:4 * i + 4, :]),
                    writes=[self.Wb[i]], stream=self.Wb[i])

    def p1_prefetch_next_w(self):
        if self.next_pass is None:
            return
        kind, hh = self.next_pass
        if kind == "g":
            self.p1_stream_w(self.wg[hh], 2056)
        else:
            self.p1_stream_w(self.wh[hh], 2048)
        self.w_loaded = True

    def p1_load_weights(self, c, wdram, ncols, nwdram):
        if not self.w_loaded:
            self.p1_stream_w(wdram, ncols)
        self.w_loaded = False
        self.op("sp", lambda e: e.dma_start(out=c["wbc"].t[:], in_=self.nmix), writes=[c["wbc"]], stream=c["wbc"])
        self.op("sp", lambda e: e.dma_start(out=c["nwbc"].t[:], in_=nwdram), writes=[c["nwbc"]], stream=c["nwbc"])
        self.op("pool", lambda e: e.memset(c["S32"].t[:], 0.0), writes=[c["S32"]])
        self.op("pool", lambda e: e.memset(c["Sbf"].t[:], 0.0), writes=[c["Sbf"]])

    def p1_front(self, c, n):
        xt = c["xt"].next()
        nb = c["nb"].next()
        nT = c["nT"].next()
        ss = c["ss"].next()
        cst = self.cst
        self.op("sp", lambda e: e.dma_start(out=xt.t[:], in_=self.x[n * 128:(n + 1) * 128, :]), writes=[xt], stream=xt)
        self.op("act", lambda e: e.activation(out=nb.t[:], in_=xt.t[:], func=AF.Square, accum_out=ss.t[:, 0:1]),
                reads=[xt], writes=[nb, ss])
        self.op("act", lambda e: e.activation(out=ss.t[:, 1:2], in_=ss.t[:, 0:1], func=AF.Ln, scale=1.0 / D, bias=self.epsn.t[:, 0:1]), reads=[ss, self.epsn], writes=[ss])
        self.op("act", lambda e: e.activation(out=ss.t[:, 2:3], in_=ss.t[:, 1:2], func=AF.Exp, scale=-0.5), reads=[ss], writes=[ss])
        self.op("dve", lambda e: e.scalar_tensor_tensor(out=nb.t[:], in0=xt.t[:], scalar=ss.t[:, 2:3], in1=c["wbc"].t[:],
                                                        op0=ALU.mult, op1=ALU.mult), reads=[xt, ss, c["wbc"]], writes=[nb])
        for g in range(2):
            tb = c["TB"].next()

            def tr(e, g=g, tb=tb):
                ins = None
                for k in range(8):
                    kc = g * 8 + k
                    ins = e.transpose(out=tb.t[:, k * 128:(k + 1) * 128], in_=nb.t[:, kc * 128:(kc + 1) * 128], identity=self.identb.t[:])
                return ins
            self.op("pe", tr, reads=[nb, self.identb], writes=[tb])
            dst = nT.t[:, g * 8:(g + 1) * 8, :].rearrange("p a b -> p (a b)")
            if g == 0:
                self.op("act", lambda e, tb=tb, dst=dst: e.activation(out=dst, in_=tb.t[:], func=AF.Copy), reads=[tb], writes=[nT])
            else:
                self.op("dve", lambda e, tb=tb, dst=dst: e.tensor_copy(out=dst, in_=tb.t[:]), reads=[tb], writes=[nT])
        return nT

    def p1_inproj_group(self, c, nT, col0, ncol):
        acc = c["A"].next()
        W = c["W"]

        def mm(e):
            ins = None
            for kc in range(16):
                ins = e.matmul(acc.t[:, 0:ncol], lhsT=nT.t[:, kc, :], rhs=W.t[:, kc, col0:col0 + ncol],
                               start=(kc == 0), stop=(kc == 15))
            return ins
        self.op("pe", mm, reads=[nT] + c["Wb"], writes=[acc])
        return acc

    def bc3(self, ap, n=128):
        return ap.unsqueeze(2).to_broadcast([128, 4, n])

    def p1_state_update(self, c, G, sd_bc_fn, lhs_fn, rhs_fn, extra_reads, extra_writes=()):
        S32, Sbf, Stmp = c["S32"], c["Sbf"], c["Stmp"]
        bk = c["GS"].next()

        def mm(e):
            ins = None
            for h in range(4):
                ins = e.matmul(bk.t[:, h * 128:(h + 1) * 128], lhsT=lhs_fn(h), rhs=rhs_fn(h), start=True, stop=True)
            return ins
        self.op("pe", mm, reads=extra_reads, writes=[bk])
        self.op("pool", lambda e: e.tensor_tensor(out=Stmp.t[:], in0=S32.t[:], in1=sd_bc_fn(), op=ALU.mult),
                reads=[S32] + extra_reads, writes=[Stmp])
        bv = bk.t[:].rearrange("p (a b) -> p a b", b=128)
        self.op("dve", lambda e: e.tensor_tensor(out=Sbf.t[:], in0=bv, in1=Stmp.t[:], op=ALU.add), reads=[bk, Stmp], writes=[Sbf])
        self.op("dve", lambda e: e.tensor_tensor(out=S32.t[:], in0=bv, in1=Stmp.t[:], op=ALU.add), reads=[bk, Stmp],
                writes=[S32] + list(extra_writes))

    def p1_output(self, c, gate, n, hd0):
        ops = c["OPS"]
        osq, ot = c["osq"], c["ot"]
        yc = c["ycol"].next()
        ytm = c["ytm"].next()
        cst = self.cst
        ov = ops.t[:].rearrange("p (a b) -> p a b", b=128)
        self.op("act", lambda e: e.activation(out=osq.t[:].rearrange("p a b -> p (a b)"), in_=ops.t[:], func=AF.Square), reads=[ops], writes=[osq])
        self.op("dve", lambda e: e.tensor_tensor(out=ot.t[:].rearrange("p a b -> p (a b)"), in0=ops.t[:], in1=gate.t[:], op=ALU.mult),
                reads=[ops, gate], writes=[ot])
        self.op("dve", lambda e: e.tensor_reduce(out=yc.t[:, 0:4], in_=osq.t[:], axis=mybir.AxisListType.X, op=ALU.add), reads=[osq], writes=[yc])
        self.op("act", lambda e: e.activation(out=yc.t[:, 4:8], in_=yc.t[:, 0:4], func=AF.Ln, scale=1.0 / 128, bias=self.epsn.t[:, 0:1]), reads=[yc, self.epsn], writes=[yc])
        self.op("act", lambda e: e.activation(out=yc.t[:, 8:12], in_=yc.t[:, 4:8], func=AF.Exp, scale=-0.5), reads=[yc], writes=[yc])
        self.op("pool", lambda e: e.tensor_tensor(out=ytm.t[:], in0=ot.t[:], in1=self.bc3(yc.t[:, 8:12]), op=ALU.mult),
                reads=[ot, yc], writes=[ytm])
        tb = c["TB"].next()
        yT = c["yT"].next()

        def tr(e):
            ins = None
            for h in range(4):
                ins = e.transpose(out=tb.t[:, h * 128:(h + 1) * 128], in_=ytm.t[:, h, :], identity=self.identb.t[:])
            return ins
        self.op("pe", tr, reads=[ytm, self.identb], writes=[tb])
        self.op("act", lambda e: e.activation(out=yT.t[:].rearrange("p a b -> p (a b)"), in_=tb.t[:, 0:512], func=AF.Copy),
                reads=[tb], writes=[yT])
        self.op("sp", lambda e: e.dma_start(out=self.yscr[n, :, hd0:hd0 + 4, :], in_=yT.t[:]),
                reads=[yT], writes=[self.b_yscr[n][hd0 // 4]], stream=yT)

    def phase1_gdn(self, hh, precast):
        with ExitStack() as st:
            c = self.p1_common_alloc(st, 2056)
            self.p1_load_weights(c, self.wg[hh], 2056, self.gnw)
            cst = self.cst
            cw = self.sb(st, "cw", [128, 12, 4], F32)
            gp = self.sb(st, "gp", [128, 8], F32)
            negA = self.sb(st, "negA", [128, 4], F32)
            self.op("sp", lambda e: e.dma_start(out=cw.t[:], in_=self.convw[hh]), writes=[cw], stream=cw)
            self.op("sp", lambda e: e.dma_start(out=gp.t[:], in_=self.gpar[hh]), writes=[gp], stream=gp)
            self.op("act", lambda e: e.activation(out=negA.t[:], in_=gp.t[:, 0:4], func=AF.Exp), reads=[gp], writes=[negA])
            self.op("dve", lambda e: e.tensor_scalar(out=negA.t[:], in0=negA.t[:], scalar1=-1.0, scalar2=None, op0=ALU.mult),
                    reads=[negA], writes=[negA])
            qkv = self.sbpool(st, "qkv", [128, 1536], F32, 1)
            zsil = self.sbpool(st, "zsil", [128, 512], F32, 1)
            zs = self.sbpool(st, "zs", [128, 512], BF16, 2)
            ab = self.sbpool(st, "ab", [128, 8], F32, 2)
            xin = self.sbpool(st, "xin", [128, 12, 131], F32, 2)
            cv = self.sb(st, "cv", [128, 12, 128], F32)
            cvg = [Buf("cv%d" % g) for g in range(12)]
            cvt = self.sbpool(st, "cvt", [128, 128], F32, 2)
            qk_s_p = self.sbpool(st, "qk_s", [128, 8, 128], F32, 1)
            vT_p = self.sbpool(st, "vT", [128, 4, 128], BF16, 1)
            sq_p = self.sbpool(st, "sq", [128, 8, 128], BF16, 1)
            rn_p = self.sbpool(st, "rn", [128, 8, 128], F32, 1)
            qkT = self.sbpool(st, "qkT", [128, 8, 128], BF16, 2)
            kv_tm_p = self.sbpool(st, "kv_tm", [128, 8, 128], BF16, 1)
            kb_p = self.sbpool(st, "kb", [128, 4, 128], BF16, 1)
            vb_p = self.sbpool(st, "vb", [128, 4, 128], BF16, 1)
            kend_p = self.sbpool(st, "kend", [128, 4, 128], BF16, 2)
            col = self.sbpool(st, "col", [128, 40], F32, 2)
            gsel = self.sbpool(st, "gsel", [128, 8], F32, 2)
            sd = self.sbpool(st, "sd", [128, 8], F32, 2)
            qgT_p = self.sbpool(st, "qgT", [128, 4, 128], BF16, 2)
            attnT_p = self.sbpool(st, "attnT", [128, 4, 128], BF16, 2)
            Tt_p = self.sbpool(st, "Tt", [128, 4, 128], BF16, 1)
            nwT_p = self.sbpool(st, "nwT", [128, 4, 128], BF16, 2)
            uv_p = self.sbpool(st, "uv", [128, 4, 128], F32, 2)
            u = self.sb(st, "u", [128, 4, 128], BF16)
            self.op("pool", lambda e: e.memset(u.t[:], 0.0), writes=[u])
            NN_p = self.sbpool(st, "NN", [128, 4, 128], F32, int(os.environ.get("MK_NPOOL", 2)))
            NT_p = self.sbpool(st, "NT", [128, 4, 128], F32, int(os.environ.get("MK_NPOOL", 2)))
            PT_p = self.sbpool(st, "PT", [128, 4, 128], F32, int(os.environ.get("MK_NPOOL", 2)))
            tA = self.sbpool(st, "tA", [128, 128], F32, 2)
            tBm = self.sbpool(st, "tBm", [128, 128], F32, 2)
            eg = self.sbpool(st, "eg", [128, 128], F32, 2)
            ETm = self.sbpool(st, "ETm", [128, 128], F32, 2)
            ETs = self.sbpool(st, "ETs", [128, 128], F32, 2)
            EPs = self.sbpool(st, "EPs", [128, 128], F32, 2)
            XX = self.sbpool(st, "XX", [128, 128], F32, 2)
            G = c["G"]
            OPS = c["OPS"]
            S32, Sbf = c["S32"], c["Sbf"]
            def f2(t):
                return t.rearrange("p a b -> p (a b)")

            state = {"xin_prev": None, "nT_next": None}

            def block(n, nT):
                tick = Buf("tick%d" % n)
                lite = n < self.npre
                lite_q = lite and n != self.npre - 1
                prev_q_valid = n >= 1 and not ((n - 1) < self.npre and (n - 1) != self.npre - 1)
                qk_s, vT, sq, rn, kv_tm = qk_s_p.next(), vT_p.next(), sq_p.next(), rn_p.next(), kv_tm_p.next()
                kb, vb, kend, qgT, attnT = kb_p.next(), vb_p.next(), kend_p.next(), qgT_p.next(), attnT_p.next()
                Tt, nwT, uv, NN, NT, PT = Tt_p.next(), nwT_p.next(), uv_p.next(), NN_p.next(), NT_p.next(), PT_p.next()
                qk = qkv.next()
                for g in range(3):
                    if lite_q and g == 0:
                        continue
                    acc = self.p1_inproj_group(c, nT, g * 512, 512)
                    if g == 1:
                        self.op("dve", lambda e, g=g, acc=acc: e.tensor_copy(out=qk.t[:, g * 512:(g + 1) * 512], in_=acc.t[:]),
                                reads=[acc], writes=[qk])
                    else:
                        self.op("act", lambda e, g=g, acc=acc: e.activation(out=qk.t[:, g * 512:(g + 1) * 512], in_=acc.t[:], func=AF.Copy),
                                reads=[acc], writes=[qk])
                z = None
                if not lite:
                    acc = self.p1_inproj_group(c, nT, 1536, 512)
                    zl = zsil.next()
                    z = zs.next()
                    self.op("act", lambda e, acc=acc: e.activation(out=zl.t[:], in_=acc.t[:], func=AF.Silu), reads=[acc], writes=[zl])
                    self.op("pool", lambda e: e.tensor_tensor(out=z.t[:], in0=zl.t[:], in1=c["nwbc"].t[:], op=ALU.mult),
                            reads=[zl, c["nwbc"]], writes=[z])
                acc = self.p1_inproj_group(c, nT, 2048, 8)
                abt = ab.next()
                self.op("dve", lambda e, acc=acc: e.tensor_copy(out=abt.t[:], in_=acc.t[:, 0:8]), reads=[acc], writes=[abt])
                if n + 1 < self.nblk:
                    state["nT_next"] = self.p1_front(c, n + 1)
                else:
                    self.p1_prefetch_next_w()

                cl = col.next()
                self.op("dve", lambda e: e.tensor_tensor(out=cl.t[:, 0:4], in0=abt.t[:, 0:4], in1=gp.t[:, 4:8], op=ALU.add),
                        reads=[abt, gp], writes=[cl])
                self.op("act", lambda e: e.activation(out=cl.t[:, 4:8], in_=cl.t[:, 0:4], func=AF.Exp), reads=[cl], writes=[cl])
                self.op("act", lambda e: e.activation(out=cl.t[:, 8:12], in_=cl.t[:, 4:8], func=AF.Ln, bias=1.0, scale=1.0), reads=[cl], writes=[cl])
                self.op("dve", lambda e: e.tensor_tensor(out=cl.t[:, 12:16], in0=cl.t[:, 8:12], in1=negA.t[:], op=ALU.mult),
                        reads=[cl, negA], writes=[cl])
                self.op("act", lambda e: e.activation(out=cl.t[:, 16:20], in_=abt.t[:, 4:8], func=AF.Sigmoid), reads=[abt], writes=[cl])
                gs = gsel.next()
                sdt = sd.next()
                self.op("pool", lambda e: e.tensor_scalar(out=gs.t[:, 0:4], in0=cl.t[:, 12:16], scalar1=cst.t[:, C_CSEL, 0:1], scalar2=None, op0=ALU.mult),
                        reads=[cl, cst], writes=[gs])
                self.op("pool", lambda e: e.tensor_scalar(out=gs.t[:, 4:8], in0=cl.t[:, 12:16], scalar1=cst.t[:, C_CSEL, 1:2], scalar2=None, op0=ALU.mult),
                        reads=[cl, cst], writes=[gs])
                bk = G.next()

                def mmG(e, bk=bk):
                    e.matmul(bk.t[:, 0:4], lhsT=cst.t[:, C_TRI, :], rhs=cl.t[:, 12:16], start=True, stop=True)
                    e.matmul(bk.t[:, 4:8], lhsT=cst.t[:, C_BD, :], rhs=cl.t[:, 12:16], start=True, stop=True)
                    return e.matmul(bk.t[:, 8:16], lhsT=cst.t[:, C_ONES, :], rhs=gs.t[:], start=True, stop=True)
                self.op("pe", mmG, reads=[cst, cl, gs], writes=[bk])
                self.op("dve", lambda e, bk=bk: e.tensor_copy(out=cl.t[:, 20:28], in_=bk.t[:, 0:8]), reads=[bk], writes=[cl])
                self.op("act", lambda e, bk=bk: e.activation(out=sdt.t[:], in_=bk.t[:, 8:16], func=AF.Exp), reads=[bk], writes=[sdt])
                self.op("act", lambda e: e.activation(out=cl.t[:, 28:32], in_=cl.t[:, 20:24], func=AF.Exp), reads=[cl], writes=[cl])
                self.op("dve", lambda e: e.tensor_tensor(out=cl.t[:, 32:36], in0=cl.t[:, 24:28], in1=cl.t[:, 20:24], op=ALU.subtract),
                        reads=[cl], writes=[cl])
                self.op("act", lambda e: e.activation(out=cl.t[:, 32:36], in_=cl.t[:, 32:36], func=AF.Exp), reads=[cl], writes=[cl])
                self.op("dve", lambda e: e.tensor_tensor(out=cl.t[:, 36:40], in0=cl.t[:, 16:20], in1=cl.t[:, 28:32], op=ALU.mult),
                        reads=[cl], writes=[cl])

                xi = xin.next()
                if state["xin_prev"] is None:
                    self.op("pool", lambda e: e.memset(xi.t[:, :, 0:3], 0.0), writes=[xi])
                else:
                    xp = state["xin_prev"]
                    gq = 0 if prev_q_valid else 4
                    self.op("pool", lambda e, xp=xp, gq=gq: e.tensor_copy(out=xi.t[:, gq:12, 0:3], in_=xp.t[:, gq:12, 128:131]), reads=[xp], writes=[xi])
                state["xin_prev"] = xi
                for gg in range(3):
                    if lite_q and gg == 0:
                        continue
                    bk = G.next()

                    def trq(e, gg=gg, bk=bk):
                        ins = None
                        for j in range(4):
                            g = gg * 4 + j
                            ins = e.transpose(out=bk.t[:, j * 128:(j + 1) * 128], in_=qk.t[:, g * 128:(g + 1) * 128], identity=cst.t[:, C_IDENT, :])
                        return ins
                    self.op("pe", trq, reads=[qk, cst], writes=[bk])
                    dst = xi.t[:, gg * 4:(gg + 1) * 4, 3:131]
                    src = bk.t[:].rearrange("p (a b) -> p a b", b=128)
                    if gg != 1:
                        self.op("act", lambda e, dst=dst, src=src: e.activation(out=dst, in_=src, func=AF.Copy), reads=[bk], writes=[xi])
                    else:
                        self.op("dve", lambda e, dst=dst, src=src: e.tensor_copy(out=dst, in_=src), reads=[bk], writes=[xi])
                npool_conv = int(os.environ.get("MK_POOLCONV", 0))
                for g in range(12):
                    if lite and g < 4:
                        continue
                    if g >= 12 - npool_conv:
                        tmpc = cvt.next()
                        self.op("pool", lambda e, g=g: e.tensor_scalar(out=cv.t[:, g, :], in0=xi.t[:, g, 0:128], scalar1=cw.t[:, g, 0:1],
                                                                       scalar2=None, op0=ALU.mult), reads=[xi, cw], writes=[cvg[g]])
                        for j in range(1, 4):
                            self.op("pool", lambda e, g=g, j=j, tmpc=tmpc: e.tensor_scalar(out=tmpc.t[:], in0=xi.t[:, g, j:j + 128], scalar1=cw.t[:, g, j:j + 1],
                                                                                           scalar2=None, op0=ALU.mult), reads=[xi, cw], writes=[tmpc])
                            self.op("pool", lambda e, g=g, tmpc=tmpc: e.tensor_tensor(out=cv.t[:, g, :], in0=cv.t[:, g, :], in1=tmpc.t[:], op=ALU.add),
                                    reads=[tmpc], writes=[cvg[g]])
                        continue
                    self.op("dve", lambda e, g=g: e.tensor_scalar(out=cv.t[:, g, :], in0=xi.t[:, g, 0:128], scalar1=cw.t[:, g, 0:1],
                                                                  scalar2=None, op0=ALU.mult), reads=[xi, cw], writes=[cvg[g]])
                    for j in range(1, 4):
                        self.op("dve", lambda e, g=g, j=j: e.scalar_tensor_tensor(out=cv.t[:, g, :], in0=xi.t[:, g, j:j + 128],
                                                                                  scalar=cw.t[:, g, j:j + 1], in1=cv.t[:, g, :],
                                                                                  op0=ALU.mult, op1=ALU.add), reads=[xi, cw], writes=[cvg[g]])
                g0 = 4 if lite else 0
                self.op("act", lambda e: e.activation(out=f2(qk_s.t[:, g0:8, :]), in_=f2(cv.t[:, g0:8, :]), func=AF.Silu), reads=cvg[g0:8], writes=[qk_s])
                self.op("act", lambda e: e.activation(out=f2(vT.t[:]), in_=f2(cv.t[:, 8:12, :]), func=AF.Silu), reads=cvg[8:12], writes=[vT])
                self.op("act", lambda e: e.activation(out=f2(sq.t[:, g0:8, :]), in_=f2(qk_s.t[:, g0:8, :]), func=AF.Square), reads=[qk_s], writes=[sq])
                qkTt = qkT.next()
                for half in range(2):
                    if lite and half == 0:
                        continue
                    bk = G.next()
                    self.op("pe", lambda e, half=half, bk=bk: e.matmul(bk.t[:], lhsT=self.onesb.t[:], rhs=f2(sq.t[:, half * 4:(half + 1) * 4, :]),
                                                                       start=True, stop=True), reads=[self.onesb, sq], writes=[bk])
                    rv = rn.t[:, half * 4:(half + 1) * 4, :]
                    self.op("act", lambda e, bk=bk, rv=rv: e.activation(out=f2(rv), in_=bk.t[:], func=AF.Ln, bias=self.epsn.t[:, 1:2]), reads=[bk, self.epsn], writes=[rn])
                    self.op("act", lambda e, rv=rv: e.activation(out=f2(rv), in_=f2(rv), func=AF.Exp, scale=-0.5), reads=[rn], writes=[rn])
                    qv = qk_s.t[:, half * 4:(half + 1) * 4, :]
                    ov = qkTt.t[:, half * 4:(half + 1) * 4, :]
                    if half == 0:
                        self.op("dve", lambda e, qv=qv, ov=ov, rv=rv: e.scalar_tensor_tensor(out=f2(ov), in0=f2(qv), scalar=128.0 ** -0.5, in1=f2(rv),
                                                                                            op0=ALU.mult, op1=ALU.mult), reads=[qk_s, rn], writes=[qkTt])
                    else:
                        self.op("dve", lambda e, qv=qv, ov=ov, rv=rv: e.tensor_tensor(out=ov, in0=qv, in1=rv, op=ALU.mult),
                                reads=[qk_s, rn], writes=[qkTt])
                tb = c["TB"].next()

                def trkv(e, tb=tb):
                    ins = None
                    for h in range(4):
                        e.transpose(out=tb.t[:, h * 128:(h + 1) * 128], in_=qkTt.t[:, 4 + h, :], identity=self.identb.t[:])
                        ins = e.transpose(out=tb.t[:, (4 + h) * 128:(5 + h) * 128], in_=vT.t[:, h, :], identity=self.identb.t[:])
                    return ins
                self.op("pe", trkv, reads=[qkTt, vT, self.identb], writes=[tb])
                self.op("act", lambda e, tb=tb: e.activation(out=f2(kv_tm.t[:]), in_=tb.t[:], func=AF.Copy), reads=[tb], writes=[kv_tm])
                self.op("pool", lambda e: e.tensor_tensor(out=kb.t[:], in0=kv_tm.t[:, 0:4, :], in1=self.bc3(cl.t[:, 36:40]), op=ALU.mult),
                        reads=[kv_tm, cl], writes=[kb])
                self.op("pool", lambda e: e.tensor_tensor(out=vb.t[:], in0=kv_tm.t[:, 4:8, :], in1=self.bc3(cl.t[:, 16:20]), op=ALU.mult),
                        reads=[kv_tm, cl], writes=[vb])
                self.op("pool", lambda e: e.tensor_tensor(out=kend.t[:], in0=kv_tm.t[:, 0:4, :], in1=self.bc3(cl.t[:, 32:36]), op=ALU.mult),
                        reads=[kv_tm, cl], writes=[kend])

                for h in range(4):
                    bk = G.next()
                    sG, sB, sK, sQ = (bk.t[:, j * 128:(j + 1) * 128] for j in range(4))

                    def mmh(e, h=h, sG=sG, sB=sB, sK=sK, sQ=sQ):
                        e.matmul(sG, lhsT=cl.t[:, 12 + h:13 + h].to_broadcast([128, 128]), rhs=cst.t[:, C_TRI, :], start=True, stop=True)
                        e.matmul(sB, lhsT=cl.t[:, 16 + h:17 + h].to_broadcast([128, 128]), rhs=cst.t[:, C_IDENT, :], start=True, stop=True)
                        ins = e.matmul(sK, lhsT=qkTt.t[:, 4 + h, :], rhs=qkTt.t[:, 4 + h, :], start=True, stop=True)
                        if lite:
                            return ins
                        return e.matmul(sQ, lhsT=qkTt.t[:, 4 + h, :], rhs=qkTt.t[:, h, :], start=True, stop=True)
                    self.op("pe", mmh, reads=[cl, cst, qkTt], writes=[bk])
                    a_, b_, g_, em, es, ep, x_ = tA.next(), tBm.next(), eg.next(), ETm.next(), ETs.next(), EPs.next(), XX.next()
                    self.op("dve", lambda e, h=h, sG=sG, a_=a_: e.tensor_scalar(out=a_.t[:], in0=sG, scalar1=cl.t[:, 20 + h:21 + h], scalar2=0.0,
                                                                                op0=ALU.subtract, op1=ALU.min), reads=[bk, cl], writes=[a_])
                    self.op("dve", lambda e, h=h, sG=sG, b_=b_: e.tensor_scalar(out=b_.t[:], in0=sG, scalar1=cl.t[:, 20 + h:21 + h], scalar2=0.0,
                                                                                op0=ALU.subtract, op1=ALU.max), reads=[bk, cl], writes=[b_])
                    if not lite:
                        self.op("act", lambda e, sG=sG, g_=g_: e.activation(out=g_.t[:], in_=sG, func=AF.Exp), reads=[bk], writes=[g_])
                    self.op("act", lambda e, a_=a_: e.activation(out=a_.t[:], in_=a_.t[:], func=AF.Exp), reads=[a_], writes=[a_])
                    self.op("act", lambda e, b_=b_: e.activation(out=b_.t[:], in_=b_.t[:], func=AF.Exp, scale=-1.0), reads=[b_], writes=[b_])
                    if not lite:
                        self.op("pool", lambda e, a_=a_, em=em: e.tensor_tensor(out=em.t[:], in0=a_.t[:], in1=cst.t[:, C_TRI, :], op=ALU.mult),
                                reads=[a_, cst], writes=[em])
                    self.op("pool", lambda e, a_=a_, es=es: e.tensor_tensor(out=es.t[:], in0=a_.t[:], in1=cst.t[:, C_NEGUS, :], op=ALU.mult),
                            reads=[a_, cst], writes=[es])
                    self.op("pool", lambda e, b_=b_, ep=ep: e.tensor_tensor(out=ep.t[:], in0=b_.t[:], in1=cst.t[:, C_NEGLS, :], op=ALU.mult),
                            reads=[b_, cst], writes=[ep])
                    if not lite:
                        self.op("pool", lambda e, h=h, g_=g_: e.tensor_tensor(out=qgT.t[:, h, :], in0=qkTt.t[:, h, :], in1=g_.t[:], op=ALU.mult),
                                reads=[qkTt, g_], writes=[qgT])
                        self.op("dve", lambda e, h=h, sQ=sQ, em=em: e.tensor_tensor(out=attnT.t[:, h, :], in0=sQ, in1=em.t[:], op=ALU.mult),
                                reads=[bk, em], writes=[attnT])
                    self.op("dve", lambda e, sK=sK, es=es, x_=x_: e.tensor_tensor(out=x_.t[:], in0=sK, in1=es.t[:], op=ALU.mult),
                            reads=[bk, es], writes=[x_])
                    self.op("dve", lambda e, h=h, sB=sB, x_=x_: e.tensor_tensor(out=NT.t[:, h, :], in0=sB, in1=x_.t[:], op=ALU.mult),
                            reads=[bk, x_], writes=[NT])
                    self.op("dve", lambda e, h=h, sK=sK, ep=ep: e.scalar_tensor_tensor(out=NN.t[:, h, :], in0=sK, scalar=cl.t[:, 16 + h:17 + h],
                                                                                       in1=ep.t[:], op0=ALU.mult, op1=ALU.mult),
                            reads=[bk, cl, ep], writes=[NN])
                self.op("pool", lambda e: e.tensor_tensor(out=PT.t[:], in0=NT.t[:], in1=cst.t[:, C_IDENT:C_IDENT + 1, :].to_broadcast([128, 4, 128]), op=ALU.add),
                        reads=[NT, cst], writes=[PT])
                GN = c["GN"]
                GS = c["GS"]
                for k in range(1, 6):
                    b1 = GN.next()

                    def mm1(e, b1=b1):
                        ins = None
                        for h in range(4):
                            ins = e.matmul(b1.t[:, h * 128:(h + 1) * 128], lhsT=NT.t[:, h, :], rhs=NN.t[:, h, :], start=True, stop=True)
                        return ins
                    self.op("pe", mm1, reads=[NT, NN], writes=[b1])
                    if k < 5:
                        b2 = GN.next()

                        def mm2(e, b2=b2):
                            ins = None
                            for h in range(4):
                                ins = e.matmul(b2.t[:, h * 128:(h + 1) * 128], lhsT=NN.t[:, h, :], rhs=NT.t[:, h, :], start=True, stop=True)
                            return ins
                        self.op("pe", mm2, reads=[NT, NN], writes=[b2])
                    self.op("act", lambda e, b1=b1: e.activation(out=f2(NN.t[:]), in_=b1.t[:], func=AF.Copy), reads=[b1], writes=[NN])
                    if k < 5:
                        self.op("dve", lambda e, b2=b2: e.tensor_copy(out=f2(NT.t[:]), in_=b2.t[:]), reads=[b2], writes=[NT])
                    b3 = GN.next()

                    def mm3(e, b3=b3):
                        ins = None
                        for h in range(4):
                            ins = e.matmul(b3.t[:, h * 128:(h + 1) * 128], lhsT=NN.t[:, h, :], rhs=PT.t[:, h, :], start=True, stop=True)
                        return ins
                    self.op("pe", mm3, reads=[NN, PT], writes=[b3])
                    if k < 5:
                        self.op("dve", lambda e, b3=b3: e.tensor_tensor(out=f2(PT.t[:]), in0=b3.t[:], in1=f2(PT.t[:]), op=ALU.add),
                                reads=[b3, PT], writes=[PT])
                    else:
                        self.op("dve", lambda e, b3=b3: e.tensor_tensor(out=f2(Tt.t[:]), in0=b3.t[:], in1=f2(PT.t[:]), op=ALU.add),
                                reads=[b3, PT], writes=[Tt])
                b1 = GN.next()
                b2 = GN.next()

                def mmw(e, b1=b1):
                    ins = None
                    for h in range(4):
                        ins = e.matmul(b1.t[:, h * 128:(h + 1) * 128], lhsT=kb.t[:, h, :], rhs=Tt.t[:, h, :], start=True, stop=True)
                    return ins

                def mmu(e, b2=b2):
                    ins = None
                    for h in range(4):
                        ins = e.matmul(b2.t[:, h * 128:(h + 1) * 128], lhsT=Tt.t[:, h, :], rhs=vb.t[:, h, :], start=True, stop=True)
                    return ins
                self.op("pe", mmw, reads=[kb, Tt], writes=[b1])
                self.op("pe", mmu, reads=[vb, Tt], writes=[b2])
                self.op("act", lambda e, b1=b1: e.activation(out=f2(nwT.t[:]), in_=b1.t[:], func=AF.Copy, scale=-1.0), reads=[b1], writes=[nwT])
                self.op("act", lambda e, b2=b2: e.activation(out=f2(uv.t[:]), in_=b2.t[:], func=AF.Copy), reads=[b2], writes=[uv])
                for cc in range(2):
                    r0, r1 = 64 * cc, 64 * cc + 64
                    bk = GS.next()

                    def mmp1(e, bk=bk, r0=r0, r1=r1):
                        ins = None
                        for h in range(4):
                            ins = e.matmul(bk.t[r0:r1, h * 128:(h + 1) * 128], lhsT=nwT.t[:, h, r0:r1], rhs=Sbf.t[:, h, :], start=True, stop=True)
                        return ins
                    self.op("pe", mmp1, reads=[nwT, Sbf], writes=[bk])
                    self.op("dve", lambda e, bk=bk, r0=r0, r1=r1: e.tensor_tensor(out=f2(u.t[r0:r1, :, :]), in0=bk.t[r0:r1, :], in1=f2(uv.t[r0:r1, :, :]), op=ALU.add),
                            reads=[bk, uv], writes=[u])

                    def mmo(e, r0=r0, r1=r1):
                        ins = None
                        for h in range(4):
                            e.matmul(OPS.t[r0:r1, h * 128:(h + 1) * 128], lhsT=qgT.t[:, h, r0:r1], rhs=Sbf.t[:, h, :], start=True, stop=False)
                            ins = e.matmul(OPS.t[r0:r1, h * 128:(h + 1) * 128], lhsT=attnT.t[:, h, r0:r1], rhs=u.t[:, h, :], start=False, stop=True)
                        return ins
                    if not lite:
                        self.op("pe", mmo, reads=[qgT, Sbf, attnT, u], writes=[OPS])
                    self.p1_state_update(c, G, lambda cc=cc: self.bc3(sdt.t[:, cc * 4:cc * 4 + 4]),
                                         lambda h, r0=r0, r1=r1: kend.t[r0:r1, h, :], lambda h, r0=r0, r1=r1: u.t[r0:r1, h, :], [kend, u, sdt],
                                         extra_writes=[tick] if cc == 1 else ())
                if not lite:
                    self.p1_output(c, z, n - self.npre, 4 * hh)
                if precast:
                    self.precast_some(4, after=tick)

            state["nT_next"] = self.p1_front(c, 0)
            for n in range(self.nblk):
                block(n, state["nT_next"])
            if precast:
                self.precast_weights()

    def phase1_hgrn(self, hh):
        with ExitStack() as st:
            c = self.p1_common_alloc(st, 2048)
            self.p1_load_weights(c, self.wh[hh], 2048, self.hnw)
            cst = self.cst
            lbr = self.sb(st, "lbr", [128, 2, 512], F32)
            lb = self.sb(st, "lb", [128, 512], F32)
            oml = self.sb(st, "oml", [128, 512], F32)
            self.op("sp", lambda e: e.dma_start(out=lbr.t[:], in_=self.lbl[hh]), writes=[lbr], stream=lbr)
            self.op("dve", lambda e: e.tensor_tensor(out=lb.t[:], in0=lbr.t[:, 0, :], in1=lbr.t[:, 1, :], op=ALU.subtract), reads=[lbr], writes=[lb])
            self.op("act", lambda e: e.activation(out=lb.t[:], in_=lb.t[:], func=AF.Sigmoid), reads=[lb], writes=[lb])
            self.op("dve", lambda e: e.tensor_scalar(out=oml.t[:], in0=lb.t[:], scalar1=-1.0, scalar2=1.0, op0=ALU.mult, op1=ALU.add),
                    reads=[lb], writes=[oml])
            qs = self.sbpool(st, "qs", [128, 512], BF16, 2)
            fo = self.sbpool(st, "fo", [128, 512], F32, 2)
            vv = self.sbpool(st, "vv", [128, 4, 128], BF16, 2)
            gsil = self.sbpool(st, "gsil", [128, 512], F32, 1)
            gs = self.sbpool(st, "gs", [128, 512], BF16, 2)
            key = self.sbpool(st, "key", [128, 4, 128], BF16, 2)
            logf = self.sbpool(st, "logf", [128, 512], F32, 2)
            kstate = self.sbpool(st, "kstate", [128, 4, 128], BF16, 2)
            qrelT = self.sb(st, "qrelT", [128, 4, 128], BF16)
            krelT = self.sb(st, "krelT", [128, 4, 128], BF16)
            AT = self.sb(st, "AT", [128, 4, 128], BF16)
            ebp = self.sbpool(st, "eb", [128, 4, 128], F32, 2)
            enbp = self.sbpool(st, "enb", [128, 4, 128], F32, 1)
            erbp = self.sbpool(st, "erb", [128, 4, 128], F32, 1)
            G = c["G"]
            OPS = c["OPS"]
            S32, Sbf = c["S32"], c["Sbf"]

            def f2(t):
                return t.rearrange("p a b -> p (a b)")

            state = {"nT_next": None}

            def block(n, nT):
                lite = n < self.npre
                q_ = qs.next()
                f_ = fo.next()
                v_ = vv.next()
                gl = gsil.next()
                g_ = gs.next()
                if not lite:
                    acc = self.p1_inproj_group(c, nT, 0, 512)
                    self.op("act", lambda e, acc=acc: e.activation(out=q_.t[:], in_=acc.t[:], func=AF.Silu), reads=[acc], writes=[q_])
                acc = self.p1_inproj_group(c, nT, 512, 512)
                self.op("act", lambda e, acc=acc: e.activation(out=f_.t[:], in_=acc.t[:], func=AF.Sigmoid), reads=[acc], writes=[f_])
                acc = self.p1_inproj_group(c, nT, 1024, 512)
                self.op("dve", lambda e, acc=acc: e.tensor_copy(out=f2(v_.t[:]), in_=acc.t[:]), reads=[acc], writes=[v_])
                if not lite:
                    acc = self.p1_inproj_group(c, nT, 1536, 512)
                    self.op("act", lambda e, acc=acc: e.activation(out=gl.t[:], in_=acc.t[:], func=AF.Silu), reads=[acc], writes=[gl])
                    self.op("pool", lambda e: e.tensor_tensor(out=g_.t[:], in0=gl.t[:], in1=c["nwbc"].t[:], op=ALU.mult),
                            reads=[gl, c["nwbc"]], writes=[g_])
                if n + 1 < self.nblk:
                    state["nT_next"] = self.p1_front(c, n + 1)
                else:
                    self.p1_prefetch_next_w()
                k_ = key.next()
                lf = logf.next()
                ks_ = kstate.next()
                eb, enb, erb = ebp.next(), enbp.next(), erbp.next()
                self.op("dve", lambda e: e.tensor_tensor(out=f_.t[:], in0=f_.t[:], in1=oml.t[:], op=ALU.mult), reads=[f_, oml], writes=[f_])
                self.op("pool", lambda e: e.tensor_tensor(out=f_.t[:], in0=f_.t[:], in1=lb.t[:], op=ALU.add), reads=[f_, lb], writes=[f_])
                self.op("pool", lambda e: e.tensor_scalar(out=f2(k_.t[:]), in0=f_.t[:], scalar1=-1.0, scalar2=1.0,
                                                          op0=ALU.mult, op1=ALU.add), reads=[f_], writes=[k_])
                self.op("act", lambda e: e.activation(out=lf.t[:], in_=f_.t[:], func=AF.Ln), reads=[f_], writes=[lf])
                b1 = G.next()
                b2 = G.next()

                def mmb(e, b1=b1):
                    ins = None
                    for h in range(4):
                        ins = e.matmul(b1.t[:, h * 128:(h + 1) * 128], lhsT=lf.t[:, h * 128:(h + 1) * 128], rhs=cst.t[:, C_TRI, :], start=True, stop=True)
                    return ins

                def mmr(e, b2=b2):
                    return e.matmul(b2.t[:], lhsT=cst.t[:, C_REVX, :], rhs=lf.t[:], start=True, stop=True)
                self.op("pe", mmb, reads=[lf, cst], writes=[b1])
                self.op("pe", mmr, reads=[lf, cst], writes=[b2])
                self.op("act", lambda e, b1=b1: e.activation(out=f2(eb.t[:]), in_=b1.t[:], func=AF.Exp), reads=[b1], writes=[eb])
                if not lite:
                    self.op("act", lambda e, b1=b1: e.activation(out=f2(enb.t[:]), in_=b1.t[:], func=AF.Exp, scale=-1.0), reads=[b1], writes=[enb])
                self.op("act", lambda e, b2=b2: e.activation(out=f2(erb.t[:]), in_=b2.t[:], func=AF.Exp), reads=[b2], writes=[erb])
                self.op("pool", lambda e: e.tensor_tensor(out=ks_.t[:], in0=k_.t[:], in1=erb.t[:], op=ALU.mult), reads=[k_, erb], writes=[ks_])
                if lite:
                    for cc in range(2):
                        r0, r1 = 64 * cc, 64 * cc + 64
                        self.p1_state_update(c, G, lambda r1=r1, eb=eb: eb.t[:, :, r1 - 1:r1].to_broadcast([128, 4, 128]),
                                             lambda h, r0=r0, r1=r1, ks_=ks_: ks_.t[r0:r1, h, :], lambda h, r0=r0, r1=r1, v_=v_: v_.t[r0:r1, h, :],
                                             [ks_, v_, eb])
                    return
                tb = c["TB"].next()

                def tr(e, tb=tb):
                    ins = None
                    for h in range(4):
                        e.transpose(out=tb.t[:, h * 128:(h + 1) * 128], in_=q_.t[:, h * 128:(h + 1) * 128], identity=self.identb.t[:])
                        ins = e.transpose(out=tb.t[:, (4 + h) * 128:(5 + h) * 128], in_=k_.t[:, h, :], identity=self.identb.t[:])
                    return ins
                self.op("pe", tr, reads=[q_, k_, self.identb], writes=[tb])
                self.op("dve", lambda e, tb=tb: e.tensor_tensor(out=f2(qrelT.t[:]), in0=tb.t[:, 0:512], in1=f2(eb.t[:]), op=ALU.mult),
                        reads=[tb, eb], writes=[qrelT])
                self.op("dve", lambda e, tb=tb: e.tensor_tensor(out=f2(krelT.t[:]), in0=tb.t[:, 512:1024], in1=f2(enb.t[:]), op=ALU.mult),
                        reads=[tb, enb], writes=[krelT])
                b3 = G.next()

                def mma(e, b3=b3):
                    ins = None
                    for h in range(4):
                        ins = e.matmul(b3.t[:, h * 128:(h + 1) * 128], lhsT=krelT.t[:, h, :], rhs=qrelT.t[:, h, :], start=True, stop=True)
                    return ins
                self.op("pe", mma, reads=[krelT, qrelT], writes=[b3])
                self.op("dve", lambda e, b3=b3: e.tensor_tensor(out=AT.t[:], in0=b3.t[:].rearrange("p (a b) -> p a b", b=128),
                                                                in1=cst.t[:, C_TRI:C_TRI + 1, :].to_broadcast([128, 4, 128]), op=ALU.mult),
                        reads=[b3, cst], writes=[AT])
                for cc in range(2):
                    r0, r1 = 64 * cc, 64 * cc + 64

                    def mmo(e, r0=r0, r1=r1):
                        ins = None
                        for h in range(4):
                            e.matmul(OPS.t[r0:r1, h * 128:(h + 1) * 128], lhsT=qrelT.t[:, h, r0:r1], rhs=Sbf.t[:, h, :], start=True, stop=False)
                            ins = e.matmul(OPS.t[r0:r1, h * 128:(h + 1) * 128], lhsT=AT.t[:, h, r0:r1], rhs=v_.t[:, h, :], start=False, stop=True)
                        return ins
                    self.op("pe", mmo, reads=[qrelT, Sbf, AT, v_], writes=[OPS])
                    self.p1_state_update(c, G, lambda r1=r1, eb=eb: eb.t[:, :, r1 - 1:r1].to_broadcast([128, 4, 128]),
                                         lambda h, r0=r0, r1=r1, ks_=ks_: ks_.t[r0:r1, h, :], lambda h, r0=r0, r1=r1, v_=v_: v_.t[r0:r1, h, :],
                                         [ks_, v_, eb])
                self.p1_output(c, g_, n - self.npre, 8 + 4 * hh)

            state["nT_next"] = self.p1_front(c, 0)
            for n in range(self.nblk):
                block(n, state["nT_next"])

    def phase2(self, th):
        with ExitStack() as st:
            cst = self.cst
            hT_sets = [[self.sb(st, "h%d_%d" % (k, i), [128, D], F32) for i in range(4)] for k in range(2)]
            n2T = self.sb(st, "n2T", [128, 16, 512], BF16)
            b_n2T = [Buf("n2T%d" % i) for i in range(4)]
            hid = st.enter_context(self.nc.sbuf_tensor("hid_%d" % th, [128, 64, 512], BF16))
            b_hid = [Buf("hid%d" % i) for i in range(64)]
            W1b = self.sbpool(st, "W1b", [128, 16, 256], BF16, 2)
            W2b = self.sbpool(st, "W2b", [128, 4, 512], BF16, 2)
            nb = self.sbpool(st, "nb2", [128, D], BF16, 1)
            rl = self.sbpool(st, "rl", [128, 512], F32, 2)
            nffn = self.sb(st, "nffn", [128, D], F32)
            nfin = self.sb(st, "nfin", [128, D], F32)
            ss = self.sbpool(st, "ss2", [128, 8], F32, 8)
            ACC = RR([self.psb(st, "ACC%d" % i, [128, 512], F32) for i in range(4)])
            A1 = RR([self.psb(st, "A1_%d" % i, [128, 512], F32) for i in range(2)])
            TB = RR([self.psb(st, "TB2_%d" % i, [128, 1024], BF16) for i in range(2)])
            self.op("sp", lambda e: e.dma_start(out=nffn.t[:], in_=self.nffn), writes=[nffn], stream=nffn)
            self.op("sp", lambda e: e.dma_start(out=nfin.t[:], in_=self.nfin), writes=[nfin], stream=nfin)
            def tile(tt):
                hT = hT_sets[tt % 2]
                tok0 = th * 2048 + tt * 512
                row0 = tt * 512
                blk0 = tok0 // 128
                for bl in range(4):
                    r = th * 2048 + row0 + bl * 128 if self.n_th == 2 else row0 + bl * 128
                    self.op("sp", lambda e, bl=bl, r=r: e.dma_start(out=hT[bl].t[:], in_=self.xres[r:r + 128, :]), writes=[hT[bl]], stream=hT[bl])
                    self.op("sp", lambda e, bl=bl: e.dma_start(out=hid[:, 0:16, bl * 128:(bl + 1) * 128], in_=self.yscr[blk0 + bl]),
                            reads=self.b_yscr[blk0 + bl], writes=b_hid[0:16], stream=b_hid[bl])
                for cg in range(8):
                    wb = W1b.next()
                    self.op("sp", lambda e, cg=cg, wb=wb: e.dma_start(out=wb.t[:], in_=self.wo_s[cg]), writes=[wb], stream=wb)
                    for bl in range(4):
                        acc = ACC.next()

                        def mm(e, bl=bl, wb=wb, acc=acc):
                            ins = None
                            for kc in range(16):
                                ins = e.matmul(acc.t[:, 0:256], lhsT=hid[:, kc, bl * 128:(bl + 1) * 128], rhs=wb.t[:, kc, :], start=(kc == 0), stop=(kc == 15))
                            return ins
                        self.op("pe", mm, reads=b_hid[0:16] + [wb], writes=[acc])
                        self.op("dve", lambda e, bl=bl, cg=cg, acc=acc: e.tensor_tensor(out=hT[bl].t[:, cg * 256:(cg + 1) * 256], in0=acc.t[:, 0:256],
                                                                                        in1=hT[bl].t[:, cg * 256:(cg + 1) * 256], op=ALU.add),
                                reads=[acc, hT[bl]], writes=[hT[bl]])
                for bl in range(4):
                    nbt = nb.next()
                    s_ = ss.next()
                    self.op("act", lambda e, bl=bl, nbt=nbt, s_=s_: e.activation(out=nbt.t[:], in_=hT[bl].t[:], func=AF.Square, accum_out=s_.t[:, 0:1]),
                            reads=[hT[bl]], writes=[nbt, s_])
                    self.op("act", lambda e, s_=s_: e.activation(out=s_.t[:, 1:2], in_=s_.t[:, 0:1], func=AF.Ln, scale=1.0 / D, bias=self.epsn.t[:, 0:1]), reads=[s_, self.epsn], writes=[s_])
                    self.op("act", lambda e, s_=s_: e.activation(out=s_.t[:, 2:3], in_=s_.t[:, 1:2], func=AF.Exp, scale=-0.5), reads=[s_], writes=[s_])
                    self.op("dve", lambda e, bl=bl, nbt=nbt, s_=s_: e.scalar_tensor_tensor(out=nbt.t[:], in0=hT[bl].t[:], scalar=s_.t[:, 2:3], in1=nffn.t[:],
                                                                                           op0=ALU.mult, op1=ALU.mult), reads=[hT[bl], s_, nffn], writes=[nbt])
                    for g in range(2):
                        tb = TB.next()

                        def tr(e, g=g, tb=tb, nbt=nbt):
                            ins = None
                            for k in range(8):
                                kc = g * 8 + k
                                ins = e.transpose(out=tb.t[:, k * 128:(k + 1) * 128], in_=nbt.t[:, kc * 128:(kc + 1) * 128], identity=self.identb.t[:])
                            return ins
                        self.op("pe", tr, reads=[nbt, self.identb], writes=[tb])
                        dst = n2T.t[:, g * 8:(g + 1) * 8, bl * 128:(bl + 1) * 128]
                        src_eng = "act" if g == 0 else "dve"
                        if src_eng == "act":
                            self.op("act", lambda e, tb=tb, dst=dst: e.activation(out=dst, in_=tb.t[:].rearrange("p (a b) -> p a b", b=128), func=AF.Copy),
                                    reads=[tb], writes=[b_n2T[bl]])
                        else:
                            self.op("dve", lambda e, tb=tb, dst=dst: e.tensor_copy(out=dst, in_=tb.t[:].rearrange("p (a b) -> p a b", b=128)),
                                    reads=[tb], writes=[b_n2T[bl]])
                for ffg in range(32):
                    wb = W1b.next()
                    self.op("sp", lambda e, ffg=ffg, wb=wb: e.dma_start(out=wb.t[:], in_=self.w1_s[ffg]), writes=[wb], stream=wb)
                    for j in range(2):
                        fb = ffg * 2 + j
                        acc = A1.next()

                        def mm(e, j=j, wb=wb, acc=acc):
                            ins = None
                            for kc in range(16):
                                ins = e.matmul(acc.t[:], lhsT=wb.t[:, kc, j * 128:(j + 1) * 128], rhs=n2T.t[:, kc, :], start=(kc == 0), stop=(kc == 15))
                            return ins
                        self.op("pe", mm, reads=[wb] + b_n2T, writes=[acc])
                        r_ = rl.next()
                        self.op("act", lambda e, acc=acc, r_=r_: e.activation(out=r_.t[:], in_=acc.t[:], func=AF.Relu), reads=[acc], writes=[r_])
                        eng = "pool" if fb % 2 == 0 else "dve"
                        self.op(eng, lambda e, fb=fb, r_=r_: e.tensor_tensor(out=hid[:, fb, :], in0=r_.t[:], in1=r_.t[:], op=ALU.mult),
                                reads=[r_], writes=[b_hid[fb]])
                s2 = [ss.next() for bl in range(4)]
                for cg in range(4):
                    accs = [ACC.next() for bl in range(4)]
                    for fbg in range(16):
                        wb = W2b.next()
                        self.op("sp", lambda e, cg=cg, fbg=fbg, wb=wb: e.dma_start(out=wb.t[:], in_=self.w2_s[cg, fbg]),
                                writes=[wb], stream=wb)

                        def mm(e, fbg=fbg, wb=wb, accs=accs):
                            ins = None
                            for f in range(4):
                                fb = fbg * 4 + f
                                for bl in range(4):
                                    ins = e.matmul(accs[bl].t[:], lhsT=hid[:, fb, bl * 128:(bl + 1) * 128], rhs=wb.t[:, f, :],
                                                   start=(fb == 0), stop=(fb == 63))
                            return ins
                        self.op("pe", mm, reads=[wb] + b_hid[fbg * 4:fbg * 4 + 4], writes=accs)
                    for bl in range(4):
                        hv = hT[bl].t[:, cg * 512:(cg + 1) * 512]
                        self.op("dve", lambda e, bl=bl, hv=hv, acc=accs[bl]: e.tensor_tensor(out=hv, in0=acc.t[:], in1=hv, op=ALU.add),
                                reads=[accs[bl], hT[bl]], writes=[hT[bl]])
                        r_ = rl.next()
                        self.op("act", lambda e, bl=bl, cg=cg, hv=hv, r_=r_: e.activation(out=r_.t[:], in_=hv, func=AF.Square, accum_out=s2[bl].t[:, cg:cg + 1]),
                                reads=[hT[bl]], writes=[r_, s2[bl]])
                for bl in range(4):
                    s_ = s2[bl]
                    self.op("dve", lambda e, s_=s_: e.tensor_reduce(out=s_.t[:, 4:5], in_=s_.t[:, 0:4], axis=mybir.AxisListType.X, op=ALU.add),
                            reads=[s_], writes=[s_])
                    self.op("act", lambda e, s_=s_: e.activation(out=s_.t[:, 5:6], in_=s_.t[:, 4:5], func=AF.Ln, scale=1.0 / D, bias=self.epsn.t[:, 0:1]), reads=[s_, self.epsn], writes=[s_])
                    self.op("act", lambda e, s_=s_: e.activation(out=s_.t[:, 6:7], in_=s_.t[:, 5:6], func=AF.Exp, scale=-0.5), reads=[s_], writes=[s_])
                    self.op("dve", lambda e, bl=bl, s_=s_: e.scalar_tensor_tensor(out=hT[bl].t[:], in0=hT[bl].t[:], scalar=s_.t[:, 6:7], in1=nfin.t[:],
                                                                                  op0=ALU.mult, op1=ALU.mult), reads=[hT[bl], s_, nfin], writes=[hT[bl]])
                    r = th * 2048 + row0 + bl * 128 if self.n_th == 2 else row0 + bl * 128
                    self.op("sp", lambda e, bl=bl, r=r: e.dma_start(out=self.out[r:r + 128, :], in_=hT[bl].t[:]), reads=[hT[bl]], writes=[], stream=hT[bl])
            for tt in range(self.ntt):
                tile(tt)
            self.op("sp", None, reads=[], writes=hT_sets[0] + hT_sets[1])


_PROG = {}


def _head_cols(base, heads, width=128):
    return np.concatenate([np.arange(base + h * width, base + (h + 1) * width) for h in heads])


def _core_inputs(inp, b, hhs, ths):
    f32 = np.float32
    w_in = inp["w_in"][0]
    m = {}
    m["x"] = np.ascontiguousarray(inp["x"][b], dtype=f32)
    m["xres"] = np.ascontiguousarray(np.concatenate([inp["x"][b, t * 2048:(t + 1) * 2048] for t in ths], axis=0), dtype=f32)
    G = 1024
    o_z, o_a, o_b = 3 * G, 4 * G, 4 * G + 8
    o_qb = 4 * G + 16
    o_f, o_i, o_g = o_qb + G, o_qb + 2 * G, o_qb + 3 * G
    for i, hh in enumerate(hhs):
        heads = list(range(4 * hh, 4 * hh + 4))
        cols = np.concatenate([_head_cols(0, heads), _head_cols(G, heads), _head_cols(2 * G, heads), _head_cols(o_z, heads),
                               o_a + np.array(heads), o_b + np.array(heads)])
        m["wg%d" % i] = np.ascontiguousarray(w_in[:, cols], dtype=f32)
        cols = np.concatenate([_head_cols(o_qb, heads), _head_cols(o_f, heads), _head_cols(o_i, heads), _head_cols(o_g, heads)])
        m["wh%d" % i] = np.ascontiguousarray(w_in[:, cols], dtype=f32)
        ccols = np.concatenate([_head_cols(0, heads), _head_cols(G, heads), _head_cols(2 * G, heads)])
        cw = inp["conv_w"][0][:, ccols]
        m["convw%d" % i] = np.ascontiguousarray(cw.reshape(4, 12, 128).transpose(2, 1, 0), dtype=f32)
        gp = np.concatenate([inp["gdn_a_log"][0][heads], inp["gdn_dt_bias"][0][heads]])[None, :]
        m["gpar%d" % i] = np.ascontiguousarray(np.broadcast_to(gp, (128, 8)), dtype=f32)
        lcols = _head_cols(0, heads)
        ll = inp["hgrn_lb_logits"][:, lcols]
        m["lbl%d" % i] = np.ascontiguousarray(np.broadcast_to(ll[None], (128, 2, 512)), dtype=f32)
    m["gnw"] = np.ascontiguousarray(np.broadcast_to(np.tile(inp["gdn_norm_w"][0], 4)[None], (128, 512)), dtype=f32)
    m["hnw"] = np.ascontiguousarray(np.broadcast_to(np.tile(inp["hgrn_norm_w"][0], 4)[None], (128, 512)), dtype=f32)
    m["wout"] = np.ascontiguousarray(inp["w_out"][0], dtype=f32)
    m["w1"] = np.ascontiguousarray(inp["w_ff1"][0], dtype=f32)
    m["w2"] = np.ascontiguousarray(inp["w_ff2"][0], dtype=f32)
    m["nmix"] = np.ascontiguousarray(np.broadcast_to(inp["norm_mix_w"][0][None], (128, D)), dtype=f32)
    m["nffn"] = np.ascontiguousarray(np.broadcast_to(inp["norm_ffn_w"][0][None], (128, D)), dtype=f32)
    m["nfin"] = np.ascontiguousarray(np.broadcast_to(inp["norm_final_w"][None], (128, D)), dtype=f32)
    m["consts"] = make_consts()
    return m


def kernel(**inputs):
    inp = {k: np.asarray(v) for k, v in inputs.items()}
    debug = bool(os.environ.get("MK_DEBUG"))
    key = ("pre8", debug)
    if key not in _PROG:
        bld = Builder(n_hh=2, n_th=1, debug=debug)
        bld.npre = 16
        _PROG[key] = bld.build()
    nc = _PROG[key]
    in_maps = []
    for c in range(8):
        b, s_ = c // 2, c % 2
        m = _core_inputs(inp, b, [0, 1], [s_])
        if s_ == 0:
            x = np.zeros((SEQ, D), np.float32)
            x[2048:] = inp["x"][b, :2048]
            m["x"] = x
        in_maps.append(m)
    res = run_bass_kernel_spmd(nc, in_maps, core_ids=list(range(8)))
    if debug:
        kernel.debug = [res.results[c] for c in range(8)]
    out = np.empty((4, SEQ, D), np.float32)
    for c in range(8):
        b, s_ = c // 2, c % 2
        out[b, s_ * 2048:(s_ + 1) * 2048] = np.asarray(res.results[c]["out"], dtype=np.float32)
    return out
```

```python
import os
from contextlib import ExitStack

import numpy as np
import ml_dtypes

import concourse.bass as bass
import concourse.mybir as mybir
from concourse.bass_utils import run_bass_kernel_spmd

F32 = mybir.dt.float32
BF16 = mybir.dt.bfloat16
AF = mybir.ActivationFunctionType
ALU = mybir.AluOpType

D = 2048
SEQ = 4096
NBLK = SEQ // 128
DFF = 8192
NORM_EPS = 1e-6
L2_EPS = 1e-6
ENGS = ("pe", "act", "dve", "pool", "sp")


class Buf:
    __slots__ = ("name", "writer", "readers", "dsem", "dcnt", "dcnt_prev", "excl")

    def __init__(self, name, excl=False):
        self.name = name
        self.excl = excl
        self.writer = None
        self.readers = []
        self.dsem = None
        self.dcnt = 0
        self.dcnt_prev = 0


class Op:
    __slots__ = ("idx", "eng", "emit", "deps", "has_dep", "inc", "sem", "is_dma", "cost", "alldeps")

    def __init__(self, idx, eng, emit, deps, is_dma):
        self.idx = idx
        self.eng = eng
        self.emit = emit
        self.deps = deps
        self.has_dep = False
        self.inc = 0
        self.sem = None
        self.is_dma = is_dma
        self.cost = 0.0
        self.alldeps = deps


class _FakeIns:
    def then_inc(self, *a, **k):
        return self


class _FakeEng:
    def __init__(self):
        self.cost = 0.0
        self.eng = "dve"

    def __getattr__(self, name):
        def call(*args, **kw):
            out = kw.get("out", args[0] if args else None)
            try:
                free = out.free_size()
            except Exception:
                free = 128
            if name == "matmul":
                rhs = kw.get("rhs")
                n = max(rhs.free_size(), 64)
                t = n / 2300.0 + 0.05
                if rhs.dtype == F32:
                    t *= float(os.environ.get("MK_F32X", 2))
            elif name == "transpose":
                t = 0.09 if kw.get("in_").dtype != F32 else 0.16
            elif name == "dma_start":
                try:
                    nb = out.nbytes()
                except Exception:
                    nb = 1 << 20
                t = 2.0 + nb / 200e3
            elif name == "activation":
                t = 0.25 + free / 1300.0
            elif self.eng == "pool":
                t = 0.3 + free / 450.0
            else:
                t = 0.12 + free / 900.0
            self.cost += t
            return _FakeIns()
        return call


class Sched:
    def __init__(self, nc, stack, same_engine_sync=True):
        self.nc = nc
        self.stack = stack
        self.ops = []
        self.base = 0
        self.same_engine_sync = same_engine_sync
        self.reorder = os.environ.get("MK_REORDER", "1") == "1"
        self.eng_sem = {e: stack.enter_context(nc.semaphore("sem_" + e)) for e in ENGS}
        self.eng_cnt = {e: 0 for e in ENGS}
        self.streams = []
        self.known = {e: {} for e in ENGS}
        self.n_wait = 0
        self.n_ops = {e: 0 for e in ENGS}

    def add(self, eng, emit, reads=(), writes=(), stream=None):
        idx = len(self.ops)
        deps = set()
        for b in reads:
            if b.writer is not None:
                deps.add(b.writer)
        for b in writes:
            if b.writer is not None:
                deps.add(b.writer)
            deps.update(b.readers)
        op = Op(idx, eng, emit, deps, stream is not None)
        op.alldeps = frozenset(deps)
        self.ops.append(op)
        for b in reads:
            b.readers.append(idx)
        for b in writes:
            b.writer = idx
            b.readers = []
        if stream is not None:
            if stream.dsem is None:
                stream.dsem = self.stack.enter_context(self.nc.semaphore("ds%d_%s" % (len(self.streams), stream.name)))
                self.streams.append(stream)
            stream.dcnt += 16
            op.sem = stream.dsem
            op.inc = stream.dcnt
        return op

    def list_schedule(self, seg, base):
        import heapq
        ops = self.ops
        fake = _FakeEng()
        for op in seg:
            if op.emit is not None:
                fake.cost = 0.0
                fake.eng = op.eng
                op.emit(fake)
                op.cost = fake.cost
            else:
                op.cost = 0.0
        idx0 = base
        n = len(seg)
        succ = [[] for _ in range(n)]
        indeg = [0] * n
        for op in seg:
            for d in op.alldeps:
                if d >= base:
                    succ[d - idx0].append(op.idx - idx0)
                    indeg[op.idx - idx0] += 1
        cp = [0.0] * n
        for i in range(n - 1, -1, -1):
            m = 0.0
            for j in succ[i]:
                if cp[j] > m:
                    m = cp[j]
            cp[i] = seg[i].cost + m
        LAT = float(os.environ.get("MK_LAT", 0.25))
        WIN = float(os.environ.get("MK_WIN", 0.3))
        eng_free = {e: 0.0 for e in ENGS}
        ready_t = [0.0] * n
        finish = [0.0] * n
        ready = {e: [] for e in ENGS}
        for i in range(n):
            if indeg[i] == 0:
                ready[seg[i].eng].append(i)
        order = []
        done = 0
        while done < n:
            best = None
            for e in ENGS:
                lst = ready[e]
                if not lst:
                    continue
                t_e = eng_free[e]
                m = min(max(t_e, ready_t[i]) for i in lst)
                cand = None
                for i in lst:
                    st = max(t_e, ready_t[i])
                    if st <= m + WIN:
                        key = (-cp[i], i)
                        if cand is None or key < cand[0]:
                            cand = (key, i, st)
                if best is None or cand[2] < best[2] - 1e-9 or (abs(cand[2] - best[2]) <= 1e-9 and cand[0] < best[0]):
                    best = cand
            _, i, st = best
            op = seg[i]
            ready[op.eng].remove(i)
            fin = st + op.cost
            if op.is_dma:
                eng_free[op.eng] = st + 0.1
            else:
                eng_free[op.eng] = fin
            finish[i] = fin
            order.append(op)
            done += 1
            for j in succ[i]:
                indeg[j] -= 1
                if fin + LAT > ready_t[j]:
                    ready_t[j] = fin + LAT
                if indeg[j] == 0:
                    ready[seg[j].eng].append(j)
        self.est_time = getattr(self, "est_time", 0.0) + (max(finish) if n else 0.0)
        if os.environ.get("MK_CHAIN") and n:
            i = max(range(n), key=lambda k: cp[k])
            chain = []
            while True:
                chain.append(i)
                if not succ[i]:
                    break
                i = max(succ[i], key=lambda k: cp[k])
            import collections
            agg = collections.OrderedDict()
            for i in chain:
                op = seg[i]
                ln = op.emit.__code__.co_firstlineno if op.emit is not None else -1
                k = (op.eng, ln)
                a = agg.setdefault(k, [0, 0.0])
                a[0] += 1
                a[1] += op.cost
            print("[chain] len=%d" % len(chain))
            for k, a in sorted(agg.items(), key=lambda kv: -kv[1][1])[:25]:
                print("   ", k, a[0], "%.1f" % a[1])
        if os.environ.get("MK_VERBOSE"):
            busy = {e: 0.0 for e in ENGS}
            for op in seg:
                if not op.is_dma:
                    busy[op.eng] += op.cost
            print("[sched] segment ops=%d est=%.0fus cp=%.0fus busy=%s" % (n, max(finish) if n else 0, max(cp) if n else 0,
                                                                     {e: int(v) for e, v in busy.items()}))
        return order

    def flush(self):
        nc = self.nc
        ops = self.ops
        base = self.base
        seg = ops[base:]
        barrier = []
        if base > 0:
            for e in ENGS:
                if self.eng_cnt[e] > 0:
                    barrier.append((self.eng_sem[e], self.eng_cnt[e]))
            for s in self.streams:
                barrier.append((s.dsem, s.dcnt_prev))
        for op in seg:
            keep = set()
            for d in op.deps:
                if d < base:
                    continue
                dop = ops[d]
                if dop.eng == op.eng and not dop.is_dma and not op.is_dma:
                    if op.eng == "pe" or not self.same_engine_sync:
                        continue
                keep.add(d)
            op.deps = keep
            for d in keep:
                ops[d].has_dep = True
        if self.reorder:
            seg = self.list_schedule(seg, base)
        by_eng = {e: [] for e in ENGS}
        for op in seg:
            by_eng[op.eng].append(op)
        for e in ENGS:
            for op in reversed(by_eng[e]):
                if not op.is_dma and op.emit is not None:
                    op.has_dep = True
                    break
        for op in seg:
            if op.is_dma or not op.has_dep or op.emit is None:
                continue
            self.eng_cnt[op.eng] += 1
            op.sem = self.eng_sem[op.eng]
            op.inc = self.eng_cnt[op.eng]

        def body_for(name):
            def body(e):
                known = self.known[name]
                for sem, val in barrier:
                    if val > 0 and known.get(id(sem), 0) < val:
                        e.wait_ge(sem, val)
                        known[id(sem)] = val
                for op in by_eng[name]:
                    need = {}
                    for d in op.deps:
                        dop = ops[d]
                        k = id(dop.sem)
                        if k not in need or need[k][1] < dop.inc:
                            need[k] = (dop.sem, dop.inc)
                    for k, (sem, val) in need.items():
                        if known.get(k, 0) < val:
                            e.wait_ge(sem, val)
                            known[k] = val
                            self.n_wait += 1
                    if op.emit is None:
                        continue
                    ins = op.emit(e)
                    self.n_ops[name] += 1
                    if op.is_dma:
                        ins.then_inc(op.sem, 16)
                    elif op.has_dep:
                        ins.then_inc(op.sem, 1)
            return body

        with nc.Block() as block:
            block.tensor(body_for("pe"))
            block.scalar(body_for("act"))
            block.vector(body_for("dve"))
            block.gpsimd(body_for("pool"))
            block.sync(body_for("sp"))
        self.base = len(ops)
        for s in self.streams:
            s.dcnt_prev = s.dcnt


class T:
    __slots__ = ("t", "b")

    def __init__(self, t, b):
        self.t = t
        self.b = b


class RR:
    def __init__(self, items):
        self.items = items
        self.i = 0

    def next(self):
        it = self.items[self.i % len(self.items)]
        self.i += 1
        return it


C_IDENT, C_TRI, C_REVX, C_NEGUS, C_NEGLS, C_BD, C_ONES, C_CSEL, C_NHALF = range(9)
NCONST = 9


def make_consts():
    c = np.zeros((128, NCONST, 128), np.float32)
    p = np.arange(128)[:, None]
    f = np.arange(128)[None, :]
    same = (p // 64) == (f // 64)
    c[:, C_IDENT] = (p == f)
    c[:, C_TRI] = same & (p <= f)
    c[:, C_REVX] = same & (p > f)
    c[:, C_NEGUS] = -1.0 * (same & (p < f))
    c[:, C_NEGLS] = -1.0 * (same & (f < p))
    c[:, C_BD] = same
    c[:, C_ONES] = 1.0
    c[:, C_CSEL, 0] = (np.arange(128) < 64)
    c[:, C_CSEL, 1] = (np.arange(128) >= 64)
    c[:, C_NHALF] = -0.5
    return c


class Builder:
    def __init__(self, n_hh, n_th, debug=False):
        self.n_hh = n_hh
        self.n_th = n_th
        self.debug = debug
        self.nc = bass.Bass("TRN2", target_bir_lowering=False)
        self.uid = 0
        self.nblk = int(os.environ.get("MK_NBLK", NBLK))
        self.npre = int(os.environ.get("MK_NPRE", 0))
        self.ntt = int(os.environ.get("MK_NTT", 4))
        self.phases = os.environ.get("MK_PH", "gh2")

    def dram_in(self, name, shape, dt=F32):
        return self.nc.dram_tensor(name, list(shape), dt, kind="ExternalInput").ap()

    def sb(self, st, name, shape, dt, nbuf=None):
        self.uid += 1
        t = st.enter_context(self.nc.sbuf_tensor("%s_%d" % (name, self.uid), list(shape), dt))
        return T(t, Buf(name))

    def sbpool(self, st, name, shape, dt, n):
        return RR([self.sb(st, "%s%d" % (name, i), shape, dt) for i in range(n)])

    def ps(self, st, name, shape, dt):
        self.uid += 1
        t = st.enter_context(self.nc.psum_tensor("%s_%d" % (name, self.uid), list(shape), dt))
        return t

    def psb(self, st, name, shape, dt):
        return T(self.ps(st, name, shape, dt), Buf(name, excl=True))

    def op(self, eng, fn, reads=(), writes=(), stream=None):
        r = [x.b if isinstance(x, T) else x for x in reads]
        w = [x.b if isinstance(x, T) else x for x in writes]
        w = w + [b for b in r if b.excl and b not in w]
        r = [b for b in r if not b.excl]
        s = stream.b if isinstance(stream, T) else stream
        return self.S.add(eng, fn, reads=r, writes=w, stream=s)

    def build(self):
        nc = self.nc
        n_hh, n_th = self.n_hh, self.n_th
        self.x = self.dram_in("x", [SEQ, D])
        self.xres = self.dram_in("xres", [n_th * 2048, D])
        self.wg = [self.dram_in("wg%d" % i, [D, 2056]) for i in range(n_hh)]
        self.wh = [self.dram_in("wh%d" % i, [D, 2048]) for i in range(n_hh)]
        self.convw = [self.dram_in("convw%d" % i, [128, 12, 4]) for i in range(n_hh)]
        self.gpar = [self.dram_in("gpar%d" % i, [128, 8]) for i in range(n_hh)]
        self.lbl = [self.dram_in("lbl%d" % i, [128, 2, 512]) for i in range(n_hh)]
        self.gnw = self.dram_in("gnw", [128, 512])
        self.hnw = self.dram_in("hnw", [128, 512])
        self.wout = self.dram_in("wout", [D, D])
        self.w1 = self.dram_in("w1", [D, DFF])
        self.w2 = self.dram_in("w2", [DFF, D])
        self.nmix = self.dram_in("nmix", [128, D])
        self.nffn = self.dram_in("nffn", [128, D])
        self.nfin = self.dram_in("nfin", [128, D])
        self.consts_d = self.dram_in("consts", [128, NCONST, 128])
        self.out = nc.dram_tensor("out", [n_th * 2048, D], F32, kind="ExternalOutput").ap()
        ykind = {"kind": "ExternalOutput"} if self.debug else {}
        n_hd = 8 * n_hh
        self.n_hd = n_hd
        self.yscr = nc.dram_tensor("yscr", [NBLK, 128, 16, 128], BF16, **ykind).ap()
        self.b_yscr = [[Buf("yscr_%d_%d" % (n, q)) for q in range(4)] for n in range(NBLK)]
        self.wo_s = nc.dram_tensor("wo_s", [8, 128, 16, 256], BF16).ap()
        self.w1_s = nc.dram_tensor("w1_s", [32, 128, 16, 256], BF16).ap()
        self.w2_s = nc.dram_tensor("w2_s", [4, 16, 128, 4, 512], BF16).ap()
        self.b_pre = Buf("precast")

        self.pre_pending = self.precast_list()
        with ExitStack() as st:
            self.S = Sched(nc, st)
            self.cst = self.sb(st, "cst", [128, NCONST, 128], F32)
            self.identb = self.sb(st, "identb", [128, 128], BF16)
            self.onesb = self.sb(st, "onesb", [128, 128], BF16)
            self.op("sp", lambda e: e.dma_start(out=self.cst.t[:], in_=self.consts_d), writes=[self.cst], stream=self.cst)
            self.op("dve", lambda e: e.tensor_copy(out=self.identb.t[:], in_=self.cst.t[:, C_IDENT, :]), reads=[self.cst], writes=[self.identb])
            self.op("dve", lambda e: e.tensor_copy(out=self.onesb.t[:], in_=self.cst.t[:, C_ONES, :]), reads=[self.cst], writes=[self.onesb])
            self.epsn = self.sb(st, "epsn", [128, 2], F32)
            self.op("pool", lambda e: e.memset(self.epsn.t[:, 0:1], NORM_EPS), writes=[self.epsn])
            self.op("pool", lambda e: e.memset(self.epsn.t[:, 1:2], L2_EPS), writes=[self.epsn])
            first = True
            with ExitStack() as p1st:
                self.Wt = self.sb(p1st, "W", [128, 16, 2056], BF16)
                self.Wb = [Buf("Wq%d" % i) for i in range(4)]
                plist = []
                for hh in range(n_hh):
                    if "g" in self.phases:
                        plist.append(("g", hh))
                    if "h" in self.phases:
                        plist.append(("h", hh))
                self.w_loaded = False
                for i, (kind, hh) in enumerate(plist):
                    nxt = plist[i + 1] if i + 1 < len(plist) else None
                    self.next_pass = nxt
                    if kind == "g":
                        self.phase1_gdn(hh, precast=first and "2" in self.phases)
                        first = False
                    else:
                        self.phase1_hgrn(hh)
                    self.S.flush()
            for th in range(n_th):
                if "2" in self.phases:
                    if first:
                        self.precast_weights()
                        first = False
                    self.phase2(th)
                    self.S.flush()
            if "2" not in self.phases:
                pass
        return nc

    def cpl(self, i):
        return self.cst.t[:, i, :]

    def precast_list(self):
        lst = []
        for g in range(8):
            lst.append((self.wo_s[g], self.wout[:, g * 256:(g + 1) * 256].rearrange("(kc p) c -> p kc c", p=128)))
        for g in range(32):
            lst.append((self.w1_s[g], self.w1[:, g * 256:(g + 1) * 256].rearrange("(kc p) c -> p kc c", p=128)))
        for i in range(16):
            for cg in range(4):
                lst.append((self.w2_s[cg, i], self.w2[i * 512:(i + 1) * 512, cg * 512:(cg + 1) * 512].rearrange("(f p) c -> p f c", p=128)))
        return lst

    def precast_some(self, k, after=None):
        for _ in range(k):
            if not self.pre_pending:
                return
            dst, src = self.pre_pending.pop(0)
            self.op("pool", lambda e, dst=dst, src=src: e.dma_start(out=dst, in_=src), reads=[after] if after is not None else [],
                    stream=self.b_pre)

    def precast_weights(self):
        pre = self.b_pre
        return self.precast_some(1000)

    def precast_weights_old(self):
        pre = self.b_pre
        for g in range(8):
            src = self.wout[:, g * 256:(g + 1) * 256].rearrange("(kc p) c -> p kc c", p=128)
            self.op("pool", lambda e, g=g, src=src: e.dma_start(out=self.wo_s[g], in_=src), stream=pre)
        for g in range(32):
            src = self.w1[:, g * 256:(g + 1) * 256].rearrange("(kc p) c -> p kc c", p=128)
            self.op("pool", lambda e, g=g, src=src: e.dma_start(out=self.w1_s[g], in_=src), stream=pre)
        for i in range(16):
            for cg in range(4):
                src = self.w2[i * 512:(i + 1) * 512, cg * 512:(cg + 1) * 512].rearrange("(f p) c -> p f c", p=128)
                self.op("pool", lambda e, i=i, cg=cg, src=src: e.dma_start(out=self.w2_s[cg, i], in_=src), stream=pre)

    def p1_common_alloc(self, st, ncols):
        c = {}
        c["W"] = self.Wt
        c["Wb"] = self.Wb
        c["wbc"] = self.sb(st, "wbc", [128, D], F32)
        c["xt"] = self.sbpool(st, "xt", [128, D], F32, 2)
        c["nb"] = self.sbpool(st, "nb", [128, D], BF16, 1)
        c["nT"] = self.sbpool(st, "nT", [128, 16, 128], BF16, 2)
        c["ss"] = self.sbpool(st, "ss", [128, 4], F32, 2)
        c["nwbc"] = self.sb(st, "nwbc", [128, 512], F32)
        if ncols == 2056:
            nA, nTB, nGF, nGN, nGS = [int(v) for v in os.environ.get("MK_BANKS", "1,1,2,2,1").split(",")]
        else:
            nA, nTB, nGF, nGN, nGS = [int(v) for v in os.environ.get("MK_BANKS_H", "2,2,2,0,1").split(",")]
        c["A"] = RR([self.psb(st, "A%d" % i, [128, 512], F32) for i in range(nA)])
        c["TB"] = RR([self.psb(st, "TB%d" % i, [128, 1024], BF16) for i in range(nTB)])
        c["GF"] = RR([self.psb(st, "GF%d" % i, [128, 512], F32) for i in range(nGF)])
        c["GN"] = RR([self.psb(st, "GN%d" % i, [128, 512], F32) for i in range(nGN)]) if nGN else c["GF"]
        c["GS"] = RR([self.psb(st, "GS%d" % i, [128, 512], F32) for i in range(nGS)]) if nGS else c["GF"]
        c["G"] = c["GF"]
        c["OPS"] = self.psb(st, "OPS", [128, 512], F32)
        c["S32"] = self.sb(st, "S32", [128, 4, 128], F32)
        c["Sbf"] = self.sb(st, "Sbf", [128, 4, 128], BF16)
        c["Stmp"] = self.sb(st, "Stmp", [128, 4, 128], F32)
        c["osq"] = self.sb(st, "osq", [128, 4, 128], BF16)
        c["ot"] = self.sb(st, "ot", [128, 4, 128], F32)
        c["ycol"] = self.sbpool(st, "ycol", [128, 12], F32, 2)
        c["ytm"] = self.sbpool(st, "ytm", [128, 4, 128], BF16, 2)
        c["yT"] = self.sbpool(st, "yT", [128, 4, 128], BF16, 2)
        return c

    def p1_stream_w(self, wdram, ncols):
        W = self.Wt
        wv = wdram.rearrange("(kc p) c -> p kc c", p=128)
        for i in range(4):
            self.op("pool", lambda e, i=i: e.dma_start(out=W.t[:, 4 * i:4 * i + 4, 0:ncols], in_=wv[:, 4 * i:4 * i + 4, :]),
                    writes=[self.Wb[i]], stream=self.Wb[i])

    def p1_prefetch_next_w(self):
        if self.next_pass is None:
            return
        kind, hh = self.next_pass
        if kind == "g":
            self.p1_stream_w(self.wg[hh], 2056)
        else:
            self.p1_stream_w(self.wh[hh], 2048)
        self.w_loaded = True

    def p1_load_weights(self, c, wdram, ncols, nwdram):
        if not self.w_loaded:
            self.p1_stream_w(wdram, ncols)
        self.w_loaded = False
        self.op("sp", lambda e: e.dma_start(out=c["wbc"].t[:], in_=self.nmix), writes=[c["wbc"]], stream=c["wbc"])
        self.op("sp", lambda e: e.dma_start(out=c["nwbc"].t[:], in_=nwdram), writes=[c["nwbc"]], stream=c["nwbc"])
        self.op("pool", lambda e: e.memset(c["S32"].t[:], 0.0), writes=[c["S32"]])
        self.op("pool", lambda e: e.memset(c["Sbf"].t[:], 0.0), writes=[c["Sbf"]])

    def p1_front(self, c, n):
        xt = c["xt"].next()
        nb = c["nb"].next()
        nT = c["nT"].next()
        ss = c["ss"].next()
        cst = self.cst
        self.op("sp", lambda e: e.dma_start(out=xt.t[:], in_=self.x[n * 128:(n + 1) * 128, :]), writes=[xt], stream=xt)
        self.op("act", lambda e: e.activation(out=nb.t[:], in_=xt.t[:], func=AF.Square, accum_out=ss.t[:, 0:1]),
                reads=[xt], writes=[nb, ss])
        self.op("act", lambda e: e.activation(out=ss.t[:, 1:2], in_=ss.t[:, 0:1], func=AF.Ln, scale=1.0 / D, bias=self.epsn.t[:, 0:1]), reads=[ss, self.epsn], writes=[ss])
        self.op("act", lambda e: e.activation(out=ss.t[:, 2:3], in_=ss.t[:, 1:2], func=AF.Exp, scale=-0.5), reads=[ss], writes=[ss])
        self.op("dve", lambda e: e.scalar_tensor_tensor(out=nb.t[:], in0=xt.t[:], scalar=ss.t[:, 2:3], in1=c["wbc"].t[:],
                                                        op0=ALU.mult, op1=ALU.mult), reads=[xt, ss, c["wbc"]], writes=[nb])
        for g in range(2):
            tb = c["TB"].next()

            def tr(e, g=g, tb=tb):
                ins = None
                for k in range(8):
                    kc = g * 8 + k
                    ins = e.transpose(out=tb.t[:, k * 128:(k + 1) * 128], in_=nb.t[:, kc * 128:(kc + 1) * 128], identity=self.identb.t[:])
                return ins
            self.op("pe", tr, reads=[nb, self.identb], writes=[tb])
            dst = nT.t[:, g * 8:(g + 1) * 8, :].rearrange("p a b -> p (a b)")
            if g == 0:
                self.op("act", lambda e, tb=tb, dst=dst: e.activation(out=dst, in_=tb.t[:], func=AF.Copy), reads=[tb], writes=[nT])
            else:
                self.op("dve", lambda e, tb=tb, dst=dst: e.tensor_copy(out=dst, in_=tb.t[:]), reads=[tb], writes=[nT])
        return nT

    def p1_inproj_group(self, c, nT, col0, ncol):
        acc = c["A"].next()
        W = c["W"]

        def mm(e):
            ins = None
            for kc in range(16):
                ins = e.matmul(acc.t[:, 0:ncol], lhsT=nT.t[:, kc, :], rhs=W.t[:, kc, col0:col0 + ncol],
                               start=(kc == 0), stop=(kc == 15))
            return ins
        self.op("pe", mm, reads=[nT] + c["Wb"], writes=[acc])
        return acc

    def bc3(self, ap, n=128):
        return ap.unsqueeze(2).to_broadcast([128, 4, n])

    def p1_state_update(self, c, G, sd_bc_fn, lhs_fn, rhs_fn, extra_reads, extra_writes=()):
        S32, Sbf, Stmp = c["S32"], c["Sbf"], c["Stmp"]
        bk = c["GS"].next()

        def mm(e):
            ins = None
            for h in range(4):
                ins = e.matmul(bk.t[:, h * 128:(h + 1) * 128], lhsT=lhs_fn(h), rhs=rhs_fn(h), start=True, stop=True)
            return ins
        self.op("pe", mm, reads=extra_reads, writes=[bk])
        self.op("pool", lambda e: e.tensor_tensor(out=Stmp.t[:], in0=S32.t[:], in1=sd_bc_fn(), op=ALU.mult),
                reads=[S32] + extra_reads, writes=[Stmp])
        bv = bk.t[:].rearrange("p (a b) -> p a b", b=128)
        self.op("dve", lambda e: e.tensor_tensor(out=Sbf.t[:], in0=bv, in1=Stmp.t[:], op=ALU.add), reads=[bk, Stmp], writes=[Sbf])
        self.op("dve", lambda e: e.tensor_tensor(out=S32.t[:], in0=bv, in1=Stmp.t[:], op=ALU.add), reads=[bk, Stmp],
                writes=[S32] + list(extra_writes))

    def p1_output(self, c, gate, n, hd0):
        ops = c["OPS"]
        osq, ot = c["osq"], c["ot"]
        yc = c["ycol"].next()
        ytm = c["ytm"].next()
        cst = self.cst
        ov = ops.t[:].rearrange("p (a b) -> p a b", b=128)
        self.op("act", lambda e: e.activation(out=osq.t[:].rearrange("p a b -> p (a b)"), in_=ops.t[:], func=AF.Square), reads=[ops], writes=[osq])
        self.op("dve", lambda e: e.tensor_tensor(out=ot.t[:].rearrange("p a b -> p (a b)"), in0=ops.t[:], in1=gate.t[:], op=ALU.mult),
                reads=[ops, gate], writes=[ot])
        self.op("dve", lambda e: e.tensor_reduce(out=yc.t[:, 0:4], in_=osq.t[:], axis=mybir.AxisListType.X, op=ALU.add), reads=[osq], writes=[yc])
        self.op("act", lambda e: e.activation(out=yc.t[:, 4:8], in_=yc.t[:, 0:4], func=AF.Ln, scale=1.0 / 128, bias=self.epsn.t[:, 0:1]), reads=[yc, self.epsn], writes=[yc])
        self.op("act", lambda e: e.activation(out=yc.t[:, 8:12], in_=yc.t[:, 4:8], func=AF.Exp, scale=-0.5), reads=[yc], writes=[yc])
        self.op("pool", lambda e: e.tensor_tensor(out=ytm.t[:], in0=ot.t[:], in1=self.bc3(yc.t[:, 8:12]), op=ALU.mult),
                reads=[ot, yc], writes=[ytm])
        tb = c["TB"].next()
        yT = c["yT"].next()

        def tr(e):
            ins = None
            for h in range(4):
                ins = e.transpose(out=tb.t[:, h * 128:(h + 1) * 128], in_=ytm.t[:, h, :], identity=self.identb.t[:])
            return ins
        self.op("pe", tr, reads=[ytm, self.identb], writes=[tb])
        self.op("act", lambda e: e.activation(out=yT.t[:].rearrange("p a b -> p (a b)"), in_=tb.t[:, 0:512], func=AF.Copy),
                reads=[tb], writes=[yT])
        self.op("sp", lambda e: e.dma_start(out=self.yscr[n, :, hd0:hd0 + 4, :], in_=yT.t[:]),
                reads=[yT], writes=[self.b_yscr[n][hd0 // 4]], stream=yT)

    def phase1_gdn(self, hh, precast):
        with ExitStack() as st:
            c = self.p1_common_alloc(st, 2056)
            self.p1_load_weights(c, self.wg[hh], 2056, self.gnw)
            cst = self.cst
            cw = self.sb(st, "cw", [128, 12, 4], F32)
            gp = self.sb(st, "gp", [128, 8], F32)
            negA = self.sb(st, "negA", [128, 4], F32)
            self.op("sp", lambda e: e.dma_start(out=cw.t[:], in_=self.convw[hh]), writes=[cw], stream=cw)
            self.op("sp", lambda e: e.dma_start(out=gp.t[:], in_=self.gpar[hh]), writes=[gp], stream=gp)
            self.op("act", lambda e: e.activation(out=negA.t[:], in_=gp.t[:, 0:4], func=AF.Exp), reads=[gp], writes=[negA])
            self.op("dve", lambda e: e.tensor_scalar(out=negA.t[:], in0=negA.t[:], scalar1=-1.0, scalar2=None, op0=ALU.mult),
                    reads=[negA], writes=[negA])
            qkv = self.sbpool(st, "qkv", [128, 1536], F32, 1)
            zsil = self.sbpool(st, "zsil", [128, 512], F32, 1)
            zs = self.sbpool(st, "zs", [128, 512], BF16, 2)
            ab = self.sbpool(st, "ab", [128, 8], F32, 2)
            xin = self.sbpool(st, "xin", [128, 12, 131], F32, 2)
            cv = self.sb(st, "cv", [128, 12, 128], F32)
            cvg = [Buf("cv%d" % g) for g in range(12)]
            cvt = self.sbpool(st, "cvt", [128, 128], F32, 2)
            qk_s_p = self.sbpool(st, "qk_s", [128, 8, 128], F32, 1)
            vT_p = self.sbpool(st, "vT", [128, 4, 128], BF16, 1)
            sq_p = self.sbpool(st, "sq", [128, 8, 128], BF16, 1)
            rn_p = self.sbpool(st, "rn", [128, 8, 128], F32, 1)
            qkT = self.sbpool(st, "qkT", [128, 8, 128], BF16, 2)
            kv_tm_p = self.sbpool(st, "kv_tm", [128, 8, 128], BF16, 1)
            kb_p = self.sbpool(st, "kb", [128, 4, 128], BF16, 1)
            vb_p = self.sbpool(st, "vb", [128, 4, 128], BF16, 1)
            kend_p = self.sbpool(st, "kend", [128, 4, 128], BF16, 2)
            col = self.sbpool(st, "col", [128, 40], F32, 2)
            gsel = self.sbpool(st, "gsel", [128, 8], F32, 2)
            sd = self.sbpool(st, "sd", [128, 8], F32, 2)
            qgT_p = self.sbpool(st, "qgT", [128, 4, 128], BF16, 2)
            attnT_p = self.sbpool(st, "attnT", [128, 4, 128], BF16, 2)
            Tt_p = self.sbpool(st, "Tt", [128, 4, 128], BF16, 1)
            nwT_p = self.sbpool(st, "nwT", [128, 4, 128], BF16, 2)
            uv_p = self.sbpool(st, "uv", [128, 4, 128], F32, 2)
            u = self.sb(st, "u", [128, 4, 128], BF16)
            self.op("pool", lambda e: e.memset(u.t[:], 0.0), writes=[u])
            NN_p = self.sbpool(st, "NN", [128, 4, 128], F32, int(os.environ.get("MK_NPOOL", 2)))
            NT_p = self.sbpool(st, "NT", [128, 4, 128], F32, int(os.environ.get("MK_NPOOL", 2)))
            PT_p = self.sbpool(st, "PT", [128, 4, 128], F32, int(os.environ.get("MK_NPOOL", 2)))
            tA = self.sbpool(st, "tA", [128, 128], F32, 2)
            tBm = self.sbpool(st, "tBm", [128, 128], F32, 2)
            eg = self.sbpool(st, "eg", [128, 128], F32, 2)
            ETm = self.sbpool(st, "ETm", [128, 128], F32, 2)
            ETs = self.sbpool(st, "ETs", [128, 128], F32, 2)
            EPs = self.sbpool(st, "EPs", [128, 128], F32, 2)
            XX = self.sbpool(st, "XX", [128, 128], F32, 2)
            G = c["G"]
            OPS = c["OPS"]
            S32, Sbf = c["S32"], c["Sbf"]
            def f2(t):
                return t.rearrange("p a b -> p (a b)")

            state = {"xin_prev": None, "nT_next": None}

            def block(n, nT):
                tick = Buf("tick%d" % n)
                lite = n < self.npre
                lite_q = lite and n != self.npre - 1
                prev_q_valid = n >= 1 and not ((n - 1) < self.npre and (n - 1) != self.npre - 1)
                qk_s, vT, sq, rn, kv_tm = qk_s_p.next(), vT_p.next(), sq_p.next(), rn_p.next(), kv_tm_p.next()
                kb, vb, kend, qgT, attnT = kb_p.next(), vb_p.next(), kend_p.next(), qgT_p.next(), attnT_p.next()
                Tt, nwT, uv, NN, NT, PT = Tt_p.next(), nwT_p.next(), uv_p.next(), NN_p.next(), NT_p.next(), PT_p.next()
                qk = qkv.next()
                for g in range(3):
                    if lite_q and g == 0:
                        continue
                    acc = self.p1_inproj_group(c, nT, g * 512, 512)
                    if g == 1:
                        self.op("dve", lambda e, g=g, acc=acc: e.tensor_copy(out=qk.t[:, g * 512:(g + 1) * 512], in_=acc.t[:]),
                                reads=[acc], writes=[qk])
                    else:
                        self.op("act", lambda e, g=g, acc=acc: e.activation(out=qk.t[:, g * 512:(g + 1) * 512], in_=acc.t[:], func=AF.Copy),
                                reads=[acc], writes=[qk])
                z = None
                if not lite:
                    acc = self.p1_inproj_group(c, nT, 1536, 512)
                    zl = zsil.next()
                    z = zs.next()
                    self.op("act", lambda e, acc=acc: e.activation(out=zl.t[:], in_=acc.t[:], func=AF.Silu), reads=[acc], writes=[zl])
                    self.op("pool", lambda e: e.tensor_tensor(out=z.t[:], in0=zl.t[:], in1=c["nwbc"].t[:], op=ALU.mult),
                            reads=[zl, c["nwbc"]], writes=[z])
                acc = self.p1_inproj_group(c, nT, 2048, 8)
                abt = ab.next()
                self.op("dve", lambda e, acc=acc: e.tensor_copy(out=abt.t[:], in_=acc.t[:, 0:8]), reads=[acc], writes=[abt])
                if n + 1 < self.nblk:
                    state["nT_next"] = self.p1_front(c, n + 1)
                else:
                    self.p1_prefetch_next_w()

                cl = col.next()
                self.op("dve", lambda e: e.tensor_tensor(out=cl.t[:, 0:4], in0=abt.t[:, 0:4], in1=gp.t[:, 4:8], op=ALU.add),
                        reads=[abt, gp], writes=[cl])
                self.op("act", lambda e: e.activation(out=cl.t[:, 4:8], in_=cl.t[:, 0:4], func=AF.Exp), reads=[cl], writes=[cl])
                self.op("act", lambda e: e.activation(out=cl.t[:, 8:12], in_=cl.t[:, 4:8], func=AF.Ln, bias=1.0, scale=1.0), reads=[cl], writes=[cl])
                self.op("dve", lambda e: e.tensor_tensor(out=cl.t[:, 12:16], in0=cl.t[:, 8:12], in1=negA.t[:], op=ALU.mult),
                        reads=[cl, negA], writes=[cl])
                self.op("act", lambda e: e.activation(out=cl.t[:, 16:20], in_=abt.t[:, 4:8], func=AF.Sigmoid), reads=[abt], writes=[cl])
                gs = gsel.next()
                sdt = sd.next()
                self.op("pool", lambda e: e.tensor_scalar(out=gs.t[:, 0:4], in0=cl.t[:, 12:16], scalar1=cst.t[:, C_CSEL, 0:1], scalar2=None, op0=ALU.mult),
                        reads=[cl, cst], writes=[gs])
                self.op("pool", lambda e: e.tensor_scalar(out=gs.t[:, 4:8], in0=cl.t[:, 12:16], scalar1=cst.t[:, C_CSEL, 1:2], scalar2=None, op0=ALU.mult),
                        reads=[cl, cst], writes=[gs])
                bk = G.next()

                def mmG(e, bk=bk):
                    e.matmul(bk.t[:, 0:4], lhsT=cst.t[:, C_TRI, :], rhs=cl.t[:, 12:16], start=True, stop=True)
                    e.matmul(bk.t[:, 4:8], lhsT=cst.t[:, C_BD, :], rhs=cl.t[:, 12:16], start=True, stop=True)
                    return e.matmul(bk.t[:, 8:16], lhsT=cst.t[:, C_ONES, :], rhs=gs.t[:], start=True, stop=True)
                self.op("pe", mmG, reads=[cst, cl, gs], writes=[bk])
                self.op("dve", lambda e, bk=bk: e.tensor_copy(out=cl.t[:, 20:28], in_=bk.t[:, 0:8]), reads=[bk], writes=[cl])
                self.op("act", lambda e, bk=bk: e.activation(out=sdt.t[:], in_=bk.t[:, 8:16], func=AF.Exp), reads=[bk], writes=[sdt])
                self.op("act", lambda e: e.activation(out=cl.t[:, 28:32], in_=cl.t[:, 20:24], func=AF.Exp), reads=[cl], writes=[cl])
                self.op("dve", lambda e: e.tensor_tensor(out=cl.t[:, 32:36], in0=cl.t[:, 24:28], in1=cl.t[:, 20:24], op=ALU.subtract),
                        reads=[cl], writes=[cl])
                self.op("act", lambda e: e.activation(out=cl.t[:, 32:36], in_=cl.t[:, 32:36], func=AF.Exp), reads=[cl], writes=[cl])
                self.op("dve", lambda e: e.tensor_tensor(out=cl.t[:, 36:40], in0=cl.t[:, 16:20], in1=cl.t[:, 28:32], op=ALU.mult),
                        reads=[cl], writes=[cl])

                xi = xin.next()
                if state["xin_prev"] is None:
                    self.op("pool", lambda e: e.memset(xi.t[:, :, 0:3], 0.0), writes=[xi])
                else:
                    xp = state["xin_prev"]
                    gq = 0 if prev_q_valid else 4
                    self.op("pool", lambda e, xp=xp, gq=gq: e.tensor_copy(out=xi.t[:, gq:12, 0:3], in_=xp.t[:, gq:12, 128:131]), reads=[xp], writes=[xi])
                state["xin_prev"] = xi
                for gg in range(3):
                    if lite_q and gg == 0:
                        continue
                    bk = G.next()

                    def trq(e, gg=gg, bk=bk):
                        ins = None
                        for j in range(4):
                            g = gg * 4 + j
                            ins = e.transpose(out=bk.t[:, j * 128:(j + 1) * 128], in_=qk.t[:, g * 128:(g + 1) * 128], identity=cst.t[:, C_IDENT, :])
                        return ins
                    self.op("pe", trq, reads=[qk, cst], writes=[bk])
                    dst = xi.t[:, gg * 4:(gg + 1) * 4, 3:131]
                    src = bk.t[:].rearrange("p (a b) -> p a b", b=128)
                    if gg != 1:
                        self.op("act", lambda e, dst=dst, src=src: e.activation(out=dst, in_=src, func=AF.Copy), reads=[bk], writes=[xi])
                    else:
                        self.op("dve", lambda e, dst=dst, src=src: e.tensor_copy(out=dst, in_=src), reads=[bk], writes=[xi])
                npool_conv = int(os.environ.get("MK_POOLCONV", 0))
                for g in range(12):
                    if lite and g < 4:
                        continue
                    if g >= 12 - npool_conv:
                        tmpc = cvt.next()
                        self.op("pool", lambda e, g=g: e.tensor_scalar(out=cv.t[:, g, :], in0=xi.t[:, g, 0:128], scalar1=cw.t[:, g, 0:1],
                                                                       scalar2=None, op0=ALU.mult), reads=[xi, cw], writes=[cvg[g]])
                        for j in range(1, 4):
                            self.op("pool", lambda e, g=g, j=j, tmpc=tmpc: e.tensor_scalar(out=tmpc.t[:], in0=xi.t[:, g, j:j + 128], scalar1=cw.t[:, g, j:j + 1],
                                                                                           scalar2=None, op0=ALU.mult), reads=[xi, cw], writes=[tmpc])
                            self.op("pool", lambda e, g=g, tmpc=tmpc: e.tensor_tensor(out=cv.t[:, g, :], in0=cv.t[:, g, :], in1=tmpc.t[:], op=ALU.add),
                                    reads=[tmpc], writes=[cvg[g]])
                        continue
                    if os.environ.get("MK_ACTTAP", "0") == "1":
                        self.op("act", lambda e, g=g: e.activation(out=cv.t[:, g, :], in_=xi.t[:, g, 0:128], func=AF.Copy, scale=cw.t[:, g, 0:1]),
                                reads=[xi, cw], writes=[cvg[g]])
                    else:
                        self.op("dve", lambda e, g=g: e.tensor_scalar(out=cv.t[:, g, :], in0=xi.t[:, g, 0:128], scalar1=cw.t[:, g, 0:1],
                                                                      scalar2=None, op0=ALU.mult), reads=[xi, cw], writes=[cvg[g]])
                    for j in range(1, 4):
                        self.op("dve", lambda e, g=g, j=j: e.scalar_tensor_tensor(out=cv.t[:, g, :], in0=xi.t[:, g, j:j + 128],
                                                                                  scalar=cw.t[:, g, j:j + 1], in1=cv.t[:, g, :],
                                                                                  op0=ALU.mult, op1=ALU.add), reads=[xi, cw], writes=[cvg[g]])
                g0 = 4 if lite else 0
                self.op("act", lambda e: e.activation(out=f2(qk_s.t[:, g0:8, :]), in_=f2(cv.t[:, g0:8, :]), func=AF.Silu), reads=cvg[g0:8], writes=[qk_s])
                self.op("act", lambda e: e.activation(out=f2(vT.t[:]), in_=f2(cv.t[:, 8:12, :]), func=AF.Silu), reads=cvg[8:12], writes=[vT])
                self.op("act", lambda e: e.activation(out=f2(sq.t[:, g0:8, :]), in_=f2(qk_s.t[:, g0:8, :]), func=AF.Square), reads=[qk_s], writes=[sq])
                qkTt = qkT.next()
                for half in range(2):
                    if lite and half == 0:
                        continue
                    bk = G.next()
                    self.op("pe", lambda e, half=half, bk=bk: e.matmul(bk.t[:], lhsT=self.onesb.t[:], rhs=f2(sq.t[:, half * 4:(half + 1) * 4, :]),
                                                                       start=True, stop=True), reads=[self.onesb, sq], writes=[bk])
                    rv = rn.t[:, half * 4:(half + 1) * 4, :]
                    self.op("act", lambda e, bk=bk, rv=rv: e.activation(out=f2(rv), in_=bk.t[:], func=AF.Ln, bias=self.epsn.t[:, 1:2]), reads=[bk, self.epsn], writes=[rn])
                    self.op("act", lambda e, rv=rv: e.activation(out=f2(rv), in_=f2(rv), func=AF.Exp, scale=-0.5), reads=[rn], writes=[rn])
                    qv = qk_s.t[:, half * 4:(half + 1) * 4, :]
                    ov = qkTt.t[:, half * 4:(half + 1) * 4, :]
                    if half == 0:
                        self.op("dve", lambda e, qv=qv, ov=ov, rv=rv: e.scalar_tensor_tensor(out=f2(ov), in0=f2(qv), scalar=128.0 ** -0.5, in1=f2(rv),
                                                                                            op0=ALU.mult, op1=ALU.mult), reads=[qk_s, rn], writes=[qkTt])
                    else:
                        self.op("dve", lambda e, qv=qv, ov=ov, rv=rv: e.tensor_tensor(out=ov, in0=qv, in1=rv, op=ALU.mult),
                                reads=[qk_s, rn], writes=[qkTt])
                tb = c["TB"].next()

                def trkv(e, tb=tb):
                    ins = None
                    for h in range(4):
                        e.transpose(out=tb.t[:, h * 128:(h + 1) * 128], in_=qkTt.t[:, 4 + h, :], identity=self.identb.t[:])
                        ins = e.transpose(out=tb.t[:, (4 + h) * 128:(5 + h) * 128], in_=vT.t[:, h, :], identity=self.identb.t[:])
                    return ins
                self.op("pe", trkv, reads=[qkTt, vT, self.identb], writes=[tb])
                self.op("act", lambda e, tb=tb: e.activation(out=f2(kv_tm.t[:]), in_=tb.t[:], func=AF.Copy), reads=[tb], writes=[kv_tm])
                self.op("pool", lambda e: e.tensor_tensor(out=kb.t[:], in0=kv_tm.t[:, 0:4, :], in1=self.bc3(cl.t[:, 36:40]), op=ALU.mult),
                        reads=[kv_tm, cl], writes=[kb])
                self.op("pool", lambda e: e.tensor_tensor(out=vb.t[:], in0=kv_tm.t[:, 4:8, :], in1=self.bc3(cl.t[:, 16:20]), op=ALU.mult),
                        reads=[kv_tm, cl], writes=[vb])
                self.op("pool", lambda e: e.tensor_tensor(out=kend.t[:], in0=kv_tm.t[:, 0:4, :], in1=self.bc3(cl.t[:, 32:36]), op=ALU.mult),
                        reads=[kv_tm, cl], writes=[kend])

                for h in range(4):
                    bk = G.next()
                    sG, sB, sK, sQ = (bk.t[:, j * 128:(j + 1) * 128] for j in range(4))

                    def mmh(e, h=h, sG=sG, sB=sB, sK=sK, sQ=sQ):
                        e.matmul(sG, lhsT=cl.t[:, 12 + h:13 + h].to_broadcast([128, 128]), rhs=cst.t[:, C_TRI, :], start=True, stop=True)
                        e.matmul(sB, lhsT=cl.t[:, 16 + h:17 + h].to_broadcast([128, 128]), rhs=cst.t[:, C_IDENT, :], start=True, stop=True)
                        ins = e.matmul(sK, lhsT=qkTt.t[:, 4 + h, :], rhs=qkTt.t[:, 4 + h, :], start=True, stop=True)
                        if lite:
                            return ins
                        return e.matmul(sQ, lhsT=qkTt.t[:, 4 + h, :], rhs=qkTt.t[:, h, :], start=True, stop=True)
                    self.op("pe", mmh, reads=[cl, cst, qkTt], writes=[bk])
                    a_, b_, g_, em, es, ep, x_ = tA.next(), tBm.next(), eg.next(), ETm.next(), ETs.next(), EPs.next(), XX.next()
                    self.op("dve", lambda e, h=h, sG=sG, a_=a_: e.tensor_scalar(out=a_.t[:], in0=sG, scalar1=cl.t[:, 20 + h:21 + h], scalar2=0.0,
                                                                                op0=ALU.subtract, op1=ALU.min), reads=[bk, cl], writes=[a_])
                    self.op("dve", lambda e, h=h, sG=sG, b_=b_: e.tensor_scalar(out=b_.t[:], in0=sG, scalar1=cl.t[:, 20 + h:21 + h], scalar2=0.0,
                                                                                op0=ALU.subtract, op1=ALU.max), reads=[bk, cl], writes=[b_])
                    if not lite:
                        self.op("act", lambda e, sG=sG, g_=g_: e.activation(out=g_.t[:], in_=sG, func=AF.Exp), reads=[bk], writes=[g_])
                    self.op("act", lambda e, a_=a_: e.activation(out=a_.t[:], in_=a_.t[:], func=AF.Exp), reads=[a_], writes=[a_])
                    self.op("act", lambda e, b_=b_: e.activation(out=b_.t[:], in_=b_.t[:], func=AF.Exp, scale=-1.0), reads=[b_], writes=[b_])
                    if not lite:
                        self.op("pool", lambda e, a_=a_, em=em: e.tensor_tensor(out=em.t[:], in0=a_.t[:], in1=cst.t[:, C_TRI, :], op=ALU.mult),
                                reads=[a_, cst], writes=[em])
                    self.op("pool", lambda e, a_=a_, es=es: e.tensor_tensor(out=es.t[:], in0=a_.t[:], in1=cst.t[:, C_NEGUS, :], op=ALU.mult),
                            reads=[a_, cst], writes=[es])
                    self.op("pool", lambda e, b_=b_, ep=ep: e.tensor_tensor(out=ep.t[:], in0=b_.t[:], in1=cst.t[:, C_NEGLS, :], op=ALU.mult),
                            reads=[b_, cst], writes=[ep])
                    if not lite:
                        self.op("pool", lambda e, h=h, g_=g_: e.tensor_tensor(out=qgT.t[:, h, :], in0=qkTt.t[:, h, :], in1=g_.t[:], op=ALU.mult),
                                reads=[qkTt, g_], writes=[qgT])
                        self.op("dve", lambda e, h=h, sQ=sQ, em=em: e.tensor_tensor(out=attnT.t[:, h, :], in0=sQ, in1=em.t[:], op=ALU.mult),
                                reads=[bk, em], writes=[attnT])
                    self.op("dve", lambda e, sK=sK, es=es, x_=x_: e.tensor_tensor(out=x_.t[:], in0=sK, in1=es.t[:], op=ALU.mult),
                            reads=[bk, es], writes=[x_])
                    self.op("dve", lambda e, h=h, sB=sB, x_=x_: e.tensor_tensor(out=NT.t[:, h, :], in0=sB, in1=x_.t[:], op=ALU.mult),
                            reads=[bk, x_], writes=[NT])
                    self.op("dve", lambda e, h=h, sK=sK, ep=ep: e.scalar_tensor_tensor(out=NN.t[:, h, :], in0=sK, scalar=cl.t[:, 16 + h:17 + h],
                                                                                       in1=ep.t[:], op0=ALU.mult, op1=ALU.mult),
                            reads=[bk, cl, ep], writes=[NN])
                self.op("pool", lambda e: e.tensor_tensor(out=PT.t[:], in0=NT.t[:], in1=cst.t[:, C_IDENT:C_IDENT + 1, :].to_broadcast([128, 4, 128]), op=ALU.add),
                        reads=[NT, cst], writes=[PT])
                GN = c["GN"]
                GS = c["GS"]
                for k in range(1, 6):
                    b1 = GN.next()

                    def mm1(e, b1=b1):
                        ins = None
                        for h in range(4):
                            ins = e.matmul(b1.t[:, h * 128:(h + 1) * 128], lhsT=NT.t[:, h, :], rhs=NN.t[:, h, :], start=True, stop=True)
                        return ins
                    self.op("pe", mm1, reads=[NT, NN], writes=[b1])
                    if k < 5:
                        b2 = GN.next()

                        def mm2(e, b2=b2):
                            ins = None
                            for h in range(4):
                                ins = e.matmul(b2.t[:, h * 128:(h + 1) * 128], lhsT=NN.t[:, h, :], rhs=NT.t[:, h, :], start=True, stop=True)
                            return ins
                        self.op("pe", mm2, reads=[NT, NN], writes=[b2])
                    self.op("act", lambda e, b1=b1: e.activation(out=f2(NN.t[:]), in_=b1.t[:], func=AF.Copy), reads=[b1], writes=[NN])
                    if k < 5:
                        if os.environ.get("MK_ACTNT", "0") == "1":
                            self.op("act", lambda e, b2=b2: e.activation(out=f2(NT.t[:]), in_=b2.t[:], func=AF.Copy), reads=[b2], writes=[NT])
                        else:
                            self.op("dve", lambda e, b2=b2: e.tensor_copy(out=f2(NT.t[:]), in_=b2.t[:]), reads=[b2], writes=[NT])
                    b3 = GN.next()

                    def mm3(e, b3=b3):
                        ins = None
                        for h in range(4):
                            ins = e.matmul(b3.t[:, h * 128:(h + 1) * 128], lhsT=NN.t[:, h, :], rhs=PT.t[:, h, :], start=True, stop=True)
                        return ins
                    self.op("pe", mm3, reads=[NN, PT], writes=[b3])
                    if k < 5:
                        self.op("dve", lambda e, b3=b3: e.tensor_tensor(out=f2(PT.t[:]), in0=b3.t[:], in1=f2(PT.t[:]), op=ALU.add),
                                reads=[b3, PT], writes=[PT])
                    else:
                        self.op("dve", lambda e, b3=b3: e.tensor_tensor(out=f2(Tt.t[:]), in0=b3.t[:], in1=f2(PT.t[:]), op=ALU.add),
                                reads=[b3, PT], writes=[Tt])
                b1 = GN.next()
                b2 = GN.next()

                def mmw(e, b1=b1):
                    ins = None
                    for h in range(4):
                        ins = e.matmul(b1.t[:, h * 128:(h + 1) * 128], lhsT=kb.t[:, h, :], rhs=Tt.t[:, h, :], start=True, stop=True)
                    return ins

                def mmu(e, b2=b2):
                    ins = None
                    for h in range(4):
                        ins = e.matmul(b2.t[:, h * 128:(h + 1) * 128], lhsT=Tt.t[:, h, :], rhs=vb.t[:, h, :], start=True, stop=True)
                    return ins
                self.op("pe", mmw, reads=[kb, Tt], writes=[b1])
                self.op("pe", mmu, reads=[vb, Tt], writes=[b2])
                self.op("act", lambda e, b1=b1: e.activation(out=f2(nwT.t[:]), in_=b1.t[:], func=AF.Copy, scale=-1.0), reads=[b1], writes=[nwT])
                self.op("act", lambda e, b2=b2: e.activation(out=f2(uv.t[:]), in_=b2.t[:], func=AF.Copy), reads=[b2], writes=[uv])
                for cc in range(2):
                    r0, r1 = 64 * cc, 64 * cc + 64
                    bk = GS.next()

                    def mmp1(e, bk=bk, r0=r0, r1=r1):
                        ins = None
                        for h in range(4):
                            ins = e.matmul(bk.t[r0:r1, h * 128:(h + 1) * 128], lhsT=nwT.t[:, h, r0:r1], rhs=Sbf.t[:, h, :], start=True, stop=True)
                        return ins
                    self.op("pe", mmp1, reads=[nwT, Sbf], writes=[bk])
                    self.op("dve", lambda e, bk=bk, r0=r0, r1=r1: e.tensor_tensor(out=f2(u.t[r0:r1, :, :]), in0=bk.t[r0:r1, :], in1=f2(uv.t[r0:r1, :, :]), op=ALU.add),
                            reads=[bk, uv], writes=[u])

                    def mmo(e, r0=r0, r1=r1):
                        ins = None
                        for h in range(4):
                            e.matmul(OPS.t[r0:r1, h * 128:(h + 1) * 128], lhsT=qgT.t[:, h, r0:r1], rhs=Sbf.t[:, h, :], start=True, stop=False)
                            ins = e.matmul(OPS.t[r0:r1, h * 128:(h + 1) * 128], lhsT=attnT.t[:, h, r0:r1], rhs=u.t[:, h, :], start=False, stop=True)
                        return ins
                    if not lite:
                        self.op("pe", mmo, reads=[qgT, Sbf, attnT, u], writes=[OPS])
                    self.p1_state_update(c, G, lambda cc=cc: self.bc3(sdt.t[:, cc * 4:cc * 4 + 4]),
                                         lambda h, r0=r0, r1=r1: kend.t[r0:r1, h, :], lambda h, r0=r0, r1=r1: u.t[r0:r1, h, :], [kend, u, sdt],
                                         extra_writes=[tick] if cc == 1 else ())
                if not lite:
                    self.p1_output(c, z, n - self.npre, 4 * hh)
                if precast:
                    self.precast_some(4, after=tick)

            state["nT_next"] = self.p1_front(c, 0)
            for n in range(self.nblk):
                block(n, state["nT_next"])
            if precast:
                self.precast_weights()

    def phase1_hgrn(self, hh):
        with ExitStack() as st:
            c = self.p1_common_alloc(st, 2048)
            self.p1_load_weights(c, self.wh[hh], 2048, self.hnw)
            cst = self.cst
            lbr = self.sb(st, "lbr", [128, 2, 512], F32)
            lb = self.sb(st, "lb", [128, 512], F32)
            oml = self.sb(st, "oml", [128, 512], F32)
            self.op("sp", lambda e: e.dma_start(out=lbr.t[:], in_=self.lbl[hh]), writes=[lbr], stream=lbr)
            self.op("dve", lambda e: e.tensor_tensor(out=lb.t[:], in0=lbr.t[:, 0, :], in1=lbr.t[:, 1, :], op=ALU.subtract), reads=[lbr], writes=[lb])
            self.op("act", lambda e: e.activation(out=lb.t[:], in_=lb.t[:], func=AF.Sigmoid), reads=[lb], writes=[lb])
            self.op("dve", lambda e: e.tensor_scalar(out=oml.t[:], in0=lb.t[:], scalar1=-1.0, scalar2=1.0, op0=ALU.mult, op1=ALU.add),
                    reads=[lb], writes=[oml])
            qs = self.sbpool(st, "qs", [128, 512], BF16, 2)
            fo = self.sbpool(st, "fo", [128, 512], F32, 2)
            vv = self.sbpool(st, "vv", [128, 4, 128], BF16, 2)
            gsil = self.sbpool(st, "gsil", [128, 512], F32, 1)
            gs = self.sbpool(st, "gs", [128, 512], BF16, 2)
            key = self.sbpool(st, "key", [128, 4, 128], BF16, 2)
            logf = self.sbpool(st, "logf", [128, 512], F32, 2)
            kstate = self.sbpool(st, "kstate", [128, 4, 128], BF16, 2)
            qrelT = self.sb(st, "qrelT", [128, 4, 128], BF16)
            krelT = self.sb(st, "krelT", [128, 4, 128], BF16)
            AT = self.sb(st, "AT", [128, 4, 128], BF16)
            ebp = self.sbpool(st, "eb", [128, 4, 128], F32, 2)
            enbp = self.sbpool(st, "enb", [128, 4, 128], F32, 1)
            erbp = self.sbpool(st, "erb", [128, 4, 128], F32, 1)
            G = c["G"]
            OPS = c["OPS"]
            S32, Sbf = c["S32"], c["Sbf"]

            def f2(t):
                return t.rearrange("p a b -> p (a b)")

            state = {"nT_next": None}

            def block(n, nT):
                lite = n < self.npre
                q_ = qs.next()
                f_ = fo.next()
                v_ = vv.next()
                gl = gsil.next()
                g_ = gs.next()
                if not lite:
                    acc = self.p1_inproj_group(c, nT, 0, 512)
                    self.op("act", lambda e, acc=acc: e.activation(out=q_.t[:], in_=acc.t[:], func=AF.Silu), reads=[acc], writes=[q_])
                acc = self.p1_inproj_group(c, nT, 512, 512)
                self.op("act", lambda e, acc=acc: e.activation(out=f_.t[:], in_=acc.t[:], func=AF.Sigmoid), reads=[acc], writes=[f_])
                acc = self.p1_inproj_group(c, nT, 1024, 512)
                self.op("dve", lambda e, acc=acc: e.tensor_copy(out=f2(v_.t[:]), in_=acc.t[:]), reads=[acc], writes=[v_])
                if not lite:
                    acc = self.p1_inproj_group(c, nT, 1536, 512)
                    self.op("act", lambda e, acc=acc: e.activation(out=gl.t[:], in_=acc.t[:], func=AF.Silu), reads=[acc], writes=[gl])
                    self.op("pool", lambda e: e.tensor_tensor(out=g_.t[:], in0=gl.t[:], in1=c["nwbc"].t[:], op=ALU.mult),
                            reads=[gl, c["nwbc"]], writes=[g_])
                if n + 1 < self.nblk:
                    state["nT_next"] = self.p1_front(c, n + 1)
                else:
                    self.p1_prefetch_next_w()
                k_ = key.next()
                lf = logf.next()
                ks_ = kstate.next()
                eb, enb, erb = ebp.next(), enbp.next(), erbp.next()
                self.op("dve", lambda e: e.tensor_tensor(out=f_.t[:], in0=f_.t[:], in1=oml.t[:], op=ALU.mult), reads=[f_, oml], writes=[f_])
                self.op("pool", lambda e: e.tensor_tensor(out=f_.t[:], in0=f_.t[:], in1=lb.t[:], op=ALU.add), reads=[f_, lb], writes=[f_])
                self.op("pool", lambda e: e.tensor_scalar(out=f2(k_.t[:]), in0=f_.t[:], scalar1=-1.0, scalar2=1.0,
                                                          op0=ALU.mult, op1=ALU.add), reads=[f_], writes=[k_])
                self.op("act", lambda e: e.activation(out=lf.t[:], in_=f_.t[:], func=AF.Ln), reads=[f_], writes=[lf])
                b1 = G.next()
                b2 = G.next()

                def mmb(e, b1=b1):
                    ins = None
                    for h in range(4):
                        ins = e.matmul(b1.t[:, h * 128:(h + 1) * 128], lhsT=lf.t[:, h * 128:(h + 1) * 128], rhs=cst.t[:, C_TRI, :], start=True, stop=True)
                    return ins

                def mmr(e, b2=b2):
                    return e.matmul(b2.t[:], lhsT=cst.t[:, C_REVX, :], rhs=lf.t[:], start=True, stop=True)
                self.op("pe", mmb, reads=[lf, cst], writes=[b1])
                self.op("pe", mmr, reads=[lf, cst], writes=[b2])
                self.op("act", lambda e, b1=b1: e.activation(out=f2(eb.t[:]), in_=b1.t[:], func=AF.Exp), reads=[b1], writes=[eb])
                if not lite:
                    self.op("act", lambda e, b1=b1: e.activation(out=f2(enb.t[:]), in_=b1.t[:], func=AF.Exp, scale=-1.0), reads=[b1], writes=[enb])
                self.op("act", lambda e, b2=b2: e.activation(out=f2(erb.t[:]), in_=b2.t[:], func=AF.Exp), reads=[b2], writes=[erb])
                self.op("pool", lambda e: e.tensor_tensor(out=ks_.t[:], in0=k_.t[:], in1=erb.t[:], op=ALU.mult), reads=[k_, erb], writes=[ks_])
                if lite:
                    for cc in range(2):
                        r0, r1 = 64 * cc, 64 * cc + 64
                        self.p1_state_update(c, G, lambda r1=r1, eb=eb: eb.t[:, :, r1 - 1:r1].to_broadcast([128, 4, 128]),
                                             lambda h, r0=r0, r1=r1, ks_=ks_: ks_.t[r0:r1, h, :], lambda h, r0=r0, r1=r1, v_=v_: v_.t[r0:r1, h, :],
                                             [ks_, v_, eb])
                    return
                tb = c["TB"].next()

                def tr(e, tb=tb):
                    ins = None
                    for h in range(4):
                        e.transpose(out=tb.t[:, h * 128:(h + 1) * 128], in_=q_.t[:, h * 128:(h + 1) * 128], identity=self.identb.t[:])
                        ins = e.transpose(out=tb.t[:, (4 + h) * 128:(5 + h) * 128], in_=k_.t[:, h, :], identity=self.identb.t[:])
                    return ins
                self.op("pe", tr, reads=[q_, k_, self.identb], writes=[tb])
                self.op("dve", lambda e, tb=tb: e.tensor_tensor(out=f2(qrelT.t[:]), in0=tb.t[:, 0:512], in1=f2(eb.t[:]), op=ALU.mult),
                        reads=[tb, eb], writes=[qrelT])
                self.op("dve", lambda e, tb=tb: e.tensor_tensor(out=f2(krelT.t[:]), in0=tb.t[:, 512:1024], in1=f2(enb.t[:]), op=ALU.mult),
                        reads=[tb, enb], writes=[krelT])
                b3 = G.next()

                def mma(e, b3=b3):
                    ins = None
                    for h in range(4):
                        ins = e.matmul(b3.t[:, h * 128:(h + 1) * 128], lhsT=krelT.t[:, h, :], rhs=qrelT.t[:, h, :], start=True, stop=True)
                    return ins
                self.op("pe", mma, reads=[krelT, qrelT], writes=[b3])
                self.op("dve", lambda e, b3=b3: e.tensor_tensor(out=AT.t[:], in0=b3.t[:].rearrange("p (a b) -> p a b", b=128),
                                                                in1=cst.t[:, C_TRI:C_TRI + 1, :].to_broadcast([128, 4, 128]), op=ALU.mult),
                        reads=[b3, cst], writes=[AT])
                for cc in range(2):
                    r0, r1 = 64 * cc, 64 * cc + 64

                    def mmo(e, r0=r0, r1=r1):
                        ins = None
                        for h in range(4):
                            e.matmul(OPS.t[r0:r1, h * 128:(h + 1) * 128], lhsT=qrelT.t[:, h, r0:r1], rhs=Sbf.t[:, h, :], start=True, stop=False)
                            ins = e.matmul(OPS.t[r0:r1, h * 128:(h + 1) * 128], lhsT=AT.t[:, h, r0:r1], rhs=v_.t[:, h, :], start=False, stop=True)
                        return ins
                    self.op("pe", mmo, reads=[qrelT, Sbf, AT, v_], writes=[OPS])
                    self.p1_state_update(c, G, lambda r1=r1, eb=eb: eb.t[:, :, r1 - 1:r1].to_broadcast([128, 4, 128]),
                                         lambda h, r0=r0, r1=r1, ks_=ks_: ks_.t[r0:r1, h, :], lambda h, r0=r0, r1=r1, v_=v_: v_.t[r0:r1, h, :],
                                         [ks_, v_, eb])
                self.p1_output(c, g_, n - self.npre, 8 + 4 * hh)

            state["nT_next"] = self.p1_front(c, 0)
            for n in range(self.nblk):
                block(n, state["nT_next"])

    def phase2(self, th):
        with ExitStack() as st:
            cst = self.cst
            hT_sets = [[self.sb(st, "h%d_%d" % (k, i), [128, D], F32) for i in range(4)] for k in range(2)]
            n2T = self.sb(st, "n2T", [128, 16, 512], BF16)
            b_n2T = [Buf("n2T%d" % i) for i in range(4)]
            hid = st.enter_context(self.nc.sbuf_tensor("hid_%d" % th, [128, 64, 512], BF16))
            b_hid = [Buf("hid%d" % i) for i in range(64)]
            W1b = self.sbpool(st, "W1b", [128, 16, 256], BF16, 2)
            W2b = self.sbpool(st, "W2b", [128, 4, 512], BF16, 2)
            nb = self.sbpool(st, "nb2", [128, D], BF16, 1)
            rl = self.sbpool(st, "rl", [128, 512], F32, 2)
            nffn = self.sb(st, "nffn", [128, D], F32)
            nfin = self.sb(st, "nfin", [128, D], F32)
            ss = self.sbpool(st, "ss2", [128, 8], F32, 8)
            ACC = RR([self.psb(st, "ACC%d" % i, [128, 512], F32) for i in range(4)])
            A1 = RR([self.psb(st, "A1_%d" % i, [128, 512], F32) for i in range(2)])
            TB = RR([self.psb(st, "TB2_%d" % i, [128, 1024], BF16) for i in range(2)])
            self.op("sp", lambda e: e.dma_start(out=nffn.t[:], in_=self.nffn), writes=[nffn], stream=nffn)
            self.op("sp", lambda e: e.dma_start(out=nfin.t[:], in_=self.nfin), writes=[nfin], stream=nfin)
            def tile(tt):
                hT = hT_sets[tt % 2]
                tok0 = th * 2048 + tt * 512
                row0 = tt * 512
                blk0 = tok0 // 128
                for bl in range(4):
                    r = th * 2048 + row0 + bl * 128 if self.n_th == 2 else row0 + bl * 128
                    self.op("sp", lambda e, bl=bl, r=r: e.dma_start(out=hT[bl].t[:], in_=self.xres[r:r + 128, :]), writes=[hT[bl]], stream=hT[bl])
                    self.op("sp", lambda e, bl=bl: e.dma_start(out=hid[:, 0:16, bl * 128:(bl + 1) * 128], in_=self.yscr[blk0 + bl]),
                            reads=self.b_yscr[blk0 + bl], writes=b_hid[0:16], stream=b_hid[bl])
                for cg in range(8):
                    wb = W1b.next()
                    self.op("sp", lambda e, cg=cg, wb=wb: e.dma_start(out=wb.t[:], in_=self.wo_s[cg]), writes=[wb], stream=wb)
                    for bl in range(4):
                        acc = ACC.next()

                        def mm(e, bl=bl, wb=wb, acc=acc):
                            ins = None
                            for kc in range(16):
                                ins = e.matmul(acc.t[:, 0:256], lhsT=hid[:, kc, bl * 128:(bl + 1) * 128], rhs=wb.t[:, kc, :], start=(kc == 0), stop=(kc == 15))
                            return ins
                        self.op("pe", mm, reads=b_hid[0:16] + [wb], writes=[acc])
                        self.op("dve", lambda e, bl=bl, cg=cg, acc=acc: e.tensor_tensor(out=hT[bl].t[:, cg * 256:(cg + 1) * 256], in0=acc.t[:, 0:256],
                                                                                        in1=hT[bl].t[:, cg * 256:(cg + 1) * 256], op=ALU.add),
                                reads=[acc, hT[bl]], writes=[hT[bl]])
                for bl in range(4):
                    nbt = nb.next()
                    s_ = ss.next()
                    self.op("act", lambda e, bl=bl, nbt=nbt, s_=s_: e.activation(out=nbt.t[:], in_=hT[bl].t[:], func=AF.Square, accum_out=s_.t[:, 0:1]),
                            reads=[hT[bl]], writes=[nbt, s_])
                    self.op("act", lambda e, s_=s_: e.activation(out=s_.t[:, 1:2], in_=s_.t[:, 0:1], func=AF.Ln, scale=1.0 / D, bias=self.epsn.t[:, 0:1]), reads=[s_, self.epsn], writes=[s_])
                    self.op("act", lambda e, s_=s_: e.activation(out=s_.t[:, 2:3], in_=s_.t[:, 1:2], func=AF.Exp, scale=-0.5), reads=[s_], writes=[s_])
                    self.op("dve", lambda e, bl=bl, nbt=nbt, s_=s_: e.scalar_tensor_tensor(out=nbt.t[:], in0=hT[bl].t[:], scalar=s_.t[:, 2:3], in1=nffn.t[:],
                                                                                           op0=ALU.mult, op1=ALU.mult), reads=[hT[bl], s_, nffn], writes=[nbt])
                    for g in range(2):
                        tb = TB.next()

                        def tr(e, g=g, tb=tb, nbt=nbt):
                            ins = None
                            for k in range(8):
                                kc = g * 8 + k
                                ins = e.transpose(out=tb.t[:, k * 128:(k + 1) * 128], in_=nbt.t[:, kc * 128:(kc + 1) * 128], identity=self.identb.t[:])
                            return ins
                        self.op("pe", tr, reads=[nbt, self.identb], writes=[tb])
                        dst = n2T.t[:, g * 8:(g + 1) * 8, bl * 128:(bl + 1) * 128]
                        src_eng = "act" if g == 0 else "dve"
                        if src_eng == "act":
                            self.op("act", lambda e, tb=tb, dst=dst: e.activation(out=dst, in_=tb.t[:].rearrange("p (a b) -> p a b", b=128), func=AF.Copy),
                                    reads=[tb], writes=[b_n2T[bl]])
                        else:
                            self.op("dve", lambda e, tb=tb, dst=dst: e.tensor_copy(out=dst, in_=tb.t[:].rearrange("p (a b) -> p a b", b=128)),
                                    reads=[tb], writes=[b_n2T[bl]])
                for ffg in range(32):
                    wb = W1b.next()
                    self.op("sp", lambda e, ffg=ffg, wb=wb: e.dma_start(out=wb.t[:], in_=self.w1_s[ffg]), writes=[wb], stream=wb)
                    for j in range(2):
                        fb = ffg * 2 + j
                        acc = A1.next()

                        def mm(e, j=j, wb=wb, acc=acc):
                            ins = None
                            for kc in range(16):
                                ins = e.matmul(acc.t[:], lhsT=wb.t[:, kc, j * 128:(j + 1) * 128], rhs=n2T.t[:, kc, :], start=(kc == 0), stop=(kc == 15))
                            return ins
                        self.op("pe", mm, reads=[wb] + b_n2T, writes=[acc])
                        r_ = rl.next()
                        self.op("act", lambda e, acc=acc, r_=r_: e.activation(out=r_.t[:], in_=acc.t[:], func=AF.Relu), reads=[acc], writes=[r_])
                        eng = "pool" if fb % 2 == 0 else "dve"
                        self.op(eng, lambda e, fb=fb, r_=r_: e.tensor_tensor(out=hid[:, fb, :], in0=r_.t[:], in1=r_.t[:], op=ALU.mult),
                                reads=[r_], writes=[b_hid[fb]])
                s2 = [ss.next() for bl in range(4)]
                for cg in range(4):
                    accs = [ACC.next() for bl in range(4)]
                    for fbg in range(16):
                        wb = W2b.next()
                        self.op("sp", lambda e, cg=cg, fbg=fbg, wb=wb: e.dma_start(out=wb.t[:], in_=self.w2_s[cg, fbg]),
                                writes=[wb], stream=wb)

                        def mm(e, fbg=fbg, wb=wb, accs=accs):
                            ins = None
                            for f in range(4):
                                fb = fbg * 4 + f
                                for bl in range(4):
                                    ins = e.matmul(accs[bl].t[:], lhsT=hid[:, fb, bl * 128:(bl + 1) * 128], rhs=wb.t[:, f, :],
                                                   start=(fb == 0), stop=(fb == 63))
                            return ins
                        self.op("pe", mm, reads=[wb] + b_hid[fbg * 4:fbg * 4 + 4], writes=accs)
                    for bl in range(4):
                        hv = hT[bl].t[:, cg * 512:(cg + 1) * 512]
                        self.op("dve", lambda e, bl=bl, hv=hv, acc=accs[bl]: e.tensor_tensor(out=hv, in0=acc.t[:], in1=hv, op=ALU.add),
                                reads=[accs[bl], hT[bl]], writes=[hT[bl]])
                        r_ = rl.next()
                        self.op("act", lambda e, bl=bl, cg=cg, hv=hv, r_=r_: e.activation(out=r_.t[:], in_=hv, func=AF.Square, accum_out=s2[bl].t[:, cg:cg + 1]),
                                reads=[hT[bl]], writes=[r_, s2[bl]])
                for bl in range(4):
                    s_ = s2[bl]
                    self.op("dve", lambda e, s_=s_: e.tensor_reduce(out=s_.t[:, 4:5], in_=s_.t[:, 0:4], axis=mybir.AxisListType.X, op=ALU.add),
                            reads=[s_], writes=[s_])
                    self.op("act", lambda e, s_=s_: e.activation(out=s_.t[:, 5:6], in_=s_.t[:, 4:5], func=AF.Ln, scale=1.0 / D, bias=self.epsn.t[:, 0:1]), reads=[s_, self.epsn], writes=[s_])
                    self.op("act", lambda e, s_=s_: e.activation(out=s_.t[:, 6:7], in_=s_.t[:, 5:6], func=AF.Exp, scale=-0.5), reads=[s_], writes=[s_])
                    self.op("dve", lambda e, bl=bl, s_=s_: e.scalar_tensor_tensor(out=hT[bl].t[:], in0=hT[bl].t[:], scalar=s_.t[:, 6:7], in1=nfin.t[:],
                                                                                  op0=ALU.mult, op1=ALU.mult), reads=[hT[bl], s_, nfin], writes=[hT[bl]])
                    r = th * 2048 + row0 + bl * 128 if self.n_th == 2 else row0 + bl * 128
                    self.op("sp", lambda e, bl=bl, r=r: e.dma_start(out=self.out[r:r + 128, :], in_=hT[bl].t[:]), reads=[hT[bl]], writes=[], stream=hT[bl])
            for tt in range(self.ntt):
                tile(tt)
            self.op("sp", None, reads=[], writes=hT_sets[0] + hT_sets[1])


_PROG = {}


def _head_cols(base, heads, width=128):
    return np.concatenate([np.arange(base + h * width, base + (h + 1) * width) for h in heads])


def _core_inputs(inp, b, hhs, ths):
    f32 = np.float32
    w_in = inp["w_in"][0]
    m = {}
    m["x"] = np.ascontiguousarray(inp["x"][b], dtype=f32)
    m["xres"] = np.ascontiguousarray(np.concatenate([inp["x"][b, t * 2048:(t + 1) * 2048] for t in ths], axis=0), dtype=f32)
    G = 1024
    o_z, o_a, o_b = 3 * G, 4 * G, 4 * G + 8
    o_qb = 4 * G + 16
    o_f, o_i, o_g = o_qb + G, o_qb + 2 * G, o_qb + 3 * G
    for i, hh in enumerate(hhs):
        heads = list(range(4 * hh, 4 * hh + 4))
        cols = np.concatenate([_head_cols(0, heads), _head_cols(G, heads), _head_cols(2 * G, heads), _head_cols(o_z, heads),
                               o_a + np.array(heads), o_b + np.array(heads)])
        m["wg%d" % i] = np.ascontiguousarray(w_in[:, cols], dtype=f32)
        cols = np.concatenate([_head_cols(o_qb, heads), _head_cols(o_f, heads), _head_cols(o_i, heads), _head_cols(o_g, heads)])
        m["wh%d" % i] = np.ascontiguousarray(w_in[:, cols], dtype=f32)
        ccols = np.concatenate([_head_cols(0, heads), _head_cols(G, heads), _head_cols(2 * G, heads)])
        cw = inp["conv_w"][0][:, ccols]
        m["convw%d" % i] = np.ascontiguousarray(cw.reshape(4, 12, 128).transpose(2, 1, 0), dtype=f32)
        gp = np.concatenate([inp["gdn_a_log"][0][heads], inp["gdn_dt_bias"][0][heads]])[None, :]
        m["gpar%d" % i] = np.ascontiguousarray(np.broadcast_to(gp, (128, 8)), dtype=f32)
        lcols = _head_cols(0, heads)
        ll = inp["hgrn_lb_logits"][:, lcols]
        m["lbl%d" % i] = np.ascontiguousarray(np.broadcast_to(ll[None], (128, 2, 512)), dtype=f32)
    m["gnw"] = np.ascontiguousarray(np.broadcast_to(np.tile(inp["gdn_norm_w"][0], 4)[None], (128, 512)), dtype=f32)
    m["hnw"] = np.ascontiguousarray(np.broadcast_to(np.tile(inp["hgrn_norm_w"][0], 4)[None], (128, 512)), dtype=f32)
    m["wout"] = np.ascontiguousarray(inp["w_out"][0], dtype=f32)
    m["w1"] = np.ascontiguousarray(inp["w_ff1"][0], dtype=f32)
    m["w2"] = np.ascontiguousarray(inp["w_ff2"][0], dtype=f32)
    m["nmix"] = np.ascontiguousarray(np.broadcast_to(inp["norm_mix_w"][0][None], (128, D)), dtype=f32)
    m["nffn"] = np.ascontiguousarray(np.broadcast_to(inp["norm_ffn_w"][0][None], (128, D)), dtype=f32)
    m["nfin"] = np.ascontiguousarray(np.broadcast_to(inp["norm_final_w"][None], (128, D)), dtype=f32)
    m["consts"] = make_consts()
    return m


def kernel(**inputs):
    inp = {k: np.asarray(v) for k, v in inputs.items()}
    debug = bool(os.environ.get("MK_DEBUG"))
    key = ("pre8", debug)
    if key not in _PROG:
        bld = Builder(n_hh=2, n_th=1, debug=debug)
        bld.npre = 16
        _PROG[key] = bld.build()
    nc = _PROG[key]
    in_maps = []
    for c in range(8):
        b, s_ = c // 2, c % 2
        m = _core_inputs(inp, b, [0, 1], [s_])
        if s_ == 0:
            x = np.zeros((SEQ, D), np.float32)
            x[2048:] = inp["x"][b, :2048]
            m["x"] = x
        in_maps.append(m)
    res = run_bass_kernel_spmd(nc, in_maps, core_ids=list(range(8)))
    if debug:
        kernel.debug = [res.results[c] for c in range(8)]
    out = np.empty((4, SEQ, D), np.float32)
    for c in range(8):
        b, s_ = c // 2, c % 2
        out[b, s_ * 2048:(s_ + 1) * 2048] = np.asarray(res.results[c]["out"], dtype=np.float32)
    return out
```
